# Optimizing a Trainium2 kernel written in Bass

```python
import functools
import jax, jax.numpy as jnp
from jax import lax
import numpy as np

D_MODEL = 1024
BATCH = 16
SEQ = 256
DEPTH = 4
DEC_BATCH = 8
DEC_SEQ = 2048
PAST_LEN = 256

GRID_W = 64
NA_HEADS = 16
HEAD_DIM = 64
D_ATTN = NA_HEADS * HEAD_DIM
NA_KH = 8
NA_KW = 16
RPB_H = 2 * NA_KH - 1
RPB_W = 2 * NA_KW - 1
D_CONV = D_MODEL
CONV_K = 31
D_SC = D_MODEL
SC_K = 3
N_BRANCH = 3
D_FF = ((8 * D_MODEL + 3 * 256 - 1) // (3 * 256)) * 256
D_IN = 2 * D_CONV + 3 * D_ATTN + 3 * D_SC + N_BRANCH * D_MODEL
Q_BLOCK = 128
EPS = 1e-6
ATTN_SCALE = HEAD_DIM ** -0.5

kernel_name = 'hybrid_dit_conformer_natten_shortconv'


def rmsnorm(x, g):
    xf = x.astype(jnp.float32)
    xf = xf * lax.rsqrt(jnp.mean(xf * xf, axis=-1, keepdims=True) + EPS)
    return xf.astype(x.dtype) * g


def layernorm(x, g, b):
    xf = x.astype(jnp.float32)
    mu = jnp.mean(xf, axis=-1, keepdims=True)
    var = jnp.mean(jnp.square(xf - mu), axis=-1, keepdims=True)
    return ((xf - mu) * lax.rsqrt(var + EPS)).astype(x.dtype) * g + b


def depthwise_conv(x, w):
    k = w.shape[0]
    pad = (k - 1) // 2
    return lax.conv_general_dilated(x, w[:, None, :], window_strides=(1,), padding=[(pad, pad)],
                                    dimension_numbers=('NWC', 'WIO', 'NWC'),
                                    feature_group_count=x.shape[-1])


def adaln(cond, w_ada, b_ada):
    m = jax.nn.silu(cond) @ w_ada + b_ada
    return jnp.split(m, 6, axis=-1)


def conformer_conv(glu_in, w_dw, b_dw, ln_g, ln_b, w_pw):
    a, g = jnp.split(glu_in, 2, axis=-1)
    h = depthwise_conv(a * jax.nn.sigmoid(g), w_dw) + b_dw
    h = jax.nn.silu(layernorm(h, ln_g, ln_b))
    return h @ w_pw


def short_conv(sc_in, w_dw, w_out):
    bg, cg, xs = jnp.split(sc_in, 3, axis=-1)
    return (bg * depthwise_conv(cg * xs, w_dw)) @ w_out


def context_attention(q, k, v):
    b, s, h, dh = q.shape
    qb = jnp.moveaxis(q.reshape(b, s // Q_BLOCK, Q_BLOCK, h, dh), 1, 0)

    def block(qi):
        sc = jnp.einsum('bqhd,bhkd->bhqk', qi, k).astype(jnp.float32) * ATTN_SCALE
        p = jax.nn.softmax(sc, axis=-1).astype(v.dtype)
        return jnp.einsum('bhqk,bhkd->bqhd', p, v)

    out = lax.map(block, qb)
    return jnp.moveaxis(out, 0, 1).reshape(b, s, h * dh)


def neighbourhood_attention(q, k, v, k_ctx, v_ctx, rpb):
    b, t, h, dh = q.shape
    rows = t // GRID_W
    kh = min(NA_KH, rows)
    n_loc = kh * NA_KW
    qg = q.reshape(b, rows, GRID_W, h, dh)
    kg = k.reshape(b, rows, GRID_W, h, dh)
    vg = v.reshape(b, rows, GRID_W, h, dh)
    cols = np.arange(GRID_W)
    col_start = np.clip(cols - NA_KW // 2, 0, GRID_W - NA_KW)
    col_idx = col_start[:, None] + np.arange(NA_KW)[None, :]
    dcol = col_idx - cols[:, None] + (NA_KW - 1)
    rpb_cols = rpb[:, :, dcol]

    def row_block(r):
        rs = jnp.clip(r - kh // 2, 0, rows - kh)
        q_r = lax.dynamic_index_in_dim(qg, r, axis=1, keepdims=False)
        k_win = lax.dynamic_slice_in_dim(kg, rs, kh, axis=1)[:, :, col_idx]
        v_win = lax.dynamic_slice_in_dim(vg, rs, kh, axis=1)[:, :, col_idx]
        drow = rs + jnp.arange(kh) - r + (NA_KH - 1)
        bias = jnp.transpose(rpb_cols[:, drow], (0, 2, 1, 3))
        s_loc = (jnp.einsum('bwhd,bkwjhd->bhwkj', q_r, k_win).astype(jnp.float32) * ATTN_SCALE
                 + bias.astype(jnp.float32))
        s_ctx = jnp.einsum('bwhd,bhld->bhwl', q_r, k_ctx).astype(jnp.float32) * ATTN_SCALE
        p = jax.nn.softmax(jnp.concatenate([s_loc.reshape(b, h, GRID_W, n_loc), s_ctx], axis=-1),
                           axis=-1).astype(v.dtype)
        p_loc = p[..., :n_loc].reshape(b, h, GRID_W, kh, NA_KW)
        return (jnp.einsum('bhwkj,bkwjhd->bwhd', p_loc, v_win)
                + jnp.einsum('bhwl,bhld->bwhd', p[..., n_loc:], v_ctx))

    out = lax.map(row_block, jnp.arange(rows))
    return jnp.moveaxis(out, 0, 1).reshape(b, t, h * dh)


def attend_context(q, k, v, rpb):
    kc = jnp.transpose(k, (0, 2, 1, 3))
    vc = jnp.transpose(v, (0, 2, 1, 3))
    return context_attention(q, kc, vc), kc, vc


def attend_latent(q, k, v, rpb, k_ctx, v_ctx):
    return neighbourhood_attention(q, k, v, k_ctx, v_ctx, rpb), None, None


def trunk_layer(x, cond, p, attend):
    shift1, scale1, gate1, shift2, scale2, gate2 = adaln(cond, p['w_ada'], p['b_ada'])
    u = rmsnorm(x, p['norm1_g']) * (1 + scale1) + shift1
    proj = u @ p['w_in']
    o1 = 2 * D_CONV
    o2 = o1 + 3 * D_ATTN
    o3 = o2 + 3 * D_SC
    glu_in, qkv, sc_in, gate_in = proj[..., :o1], proj[..., o1:o2], proj[..., o2:o3], proj[..., o3:]
    y_conv = conformer_conv(glu_in, p['conv_dw_w'], p['conv_dw_b'], p['conv_ln_g'], p['conv_ln_b'], p['conv_pw_w'])
    y_sc = short_conv(sc_in, p['sc_dw_w'], p['sc_out_w'])
    b, t = x.shape[0], x.shape[1]
    q, k, v = [z.reshape(b, t, NA_HEADS, HEAD_DIM) for z in jnp.split(qkv, 3, axis=-1)]
    y_attn, k_keep, v_keep = attend(q, k, v, p['na_rpb'])
    y_na = y_attn @ p['na_out_w']
    g_conv, g_na, g_sc = jnp.split(jax.nn.sigmoid(gate_in), 3, axis=-1)
    mixed = (g_conv * y_conv + g_na * y_na + g_sc * y_sc) @ p['w_o']
    x = x + gate1 * mixed
    u2 = rmsnorm(x, p['norm2_g']) * (1 + scale2) + shift2
    ffn = (jax.nn.silu(u2 @ p['ffn_w_gate']) * (u2 @ p['ffn_w_up'])) @ p['ffn_w_down']
    x = x + gate2 * ffn
    return x, k_keep, v_keep


def setup_inputs(seed: int = 0) -> dict:
    key = jax.random.key(seed)
    ks = jax.random.split(key, 26)

    def nrm(k, shape, scale):
        return jax.random.normal(k, shape, jnp.float32) * scale

    f = D_MODEL ** -0.5
    return {
        'x_prompt': nrm(ks[0], (BATCH, SEQ, D_MODEL), 1.0),
        'x_sample': nrm(ks[1], (DEC_BATCH, DEC_SEQ, D_MODEL), 1.0),
        'cache_k': nrm(ks[2], (DEC_BATCH, DEPTH, NA_HEADS, PAST_LEN, HEAD_DIM), 1.0),
        'cache_v': nrm(ks[3], (DEC_BATCH, DEPTH, NA_HEADS, PAST_LEN, HEAD_DIM), 1.0),
        'c': nrm(ks[4], (DEC_BATCH, D_MODEL), 1.0),
        'c_ctx': nrm(ks[5], (D_MODEL,), 1.0),
        'w_ada': nrm(ks[6], (DEPTH, D_MODEL, 6 * D_MODEL), 0.5 * f),
        'b_ada': nrm(ks[7], (DEPTH, 6 * D_MODEL), 0.02),
        'norm1_g': 1.0 + nrm(ks[8], (DEPTH, D_MODEL), 0.02),
        'w_in': nrm(ks[9], (DEPTH, D_MODEL, D_IN), f),
        'conv_dw_w': nrm(ks[10], (DEPTH, CONV_K, D_CONV), CONV_K ** -0.5),
        'conv_dw_b': nrm(ks[11], (DEPTH, D_CONV), 0.02),
        'conv_ln_g': 1.0 + nrm(ks[12], (DEPTH, D_CONV), 0.02),
        'conv_ln_b': nrm(ks[13], (DEPTH, D_CONV), 0.02),
        'conv_pw_w': nrm(ks[14], (DEPTH, D_CONV, D_MODEL), D_CONV ** -0.5),
        'sc_dw_w': nrm(ks[15], (DEPTH, SC_K, D_SC), SC_K ** -0.5),
        'sc_out_w': nrm(ks[16], (DEPTH, D_SC, D_MODEL), D_SC ** -0.5),
        'na_rpb': nrm(ks[17], (DEPTH, NA_HEADS, RPB_H, RPB_W), 0.1),
        'na_out_w': nrm(ks[18], (DEPTH, D_ATTN, D_MODEL), D_ATTN ** -0.5),
        'w_o': nrm(ks[19], (DEPTH, D_MODEL, D_MODEL), f),
        'norm2_g': 1.0 + nrm(ks[20], (DEPTH, D_MODEL), 0.02),
        'ffn_w_gate': nrm(ks[21], (DEPTH, D_MODEL, D_FF), f),
        'ffn_w_up': nrm(ks[22], (DEPTH, D_MODEL, D_FF), f),
        'ffn_w_down': nrm(ks[23], (DEPTH, D_FF, D_MODEL), D_FF ** -0.5),
        'final_g': 1.0 + nrm(ks[24], (D_MODEL,), 0.02),
    }


def reference(x_prompt, x_sample, cache_k, cache_v, c, c_ctx, w_ada, b_ada, norm1_g, w_in,
              conv_dw_w, conv_dw_b, conv_ln_g, conv_ln_b, conv_pw_w, sc_dw_w, sc_out_w,
              na_rpb, na_out_w, w_o, norm2_g, ffn_w_gate, ffn_w_up, ffn_w_down, final_g):
    cond_ctx = c_ctx[None, None, :]
    cond_lat = c[:, None, :]
    xp = x_prompt
    xs = x_sample
    new_k_list = []
    new_v_list = []
    for l in range(DEPTH):
        p = {
            'w_ada': w_ada[l], 'b_ada': b_ada[l], 'norm1_g': norm1_g[l], 'w_in': w_in[l],
            'conv_dw_w': conv_dw_w[l], 'conv_dw_b': conv_dw_b[l], 'conv_ln_g': conv_ln_g[l],
            'conv_ln_b': conv_ln_b[l], 'conv_pw_w': conv_pw_w[l], 'sc_dw_w': sc_dw_w[l],
            'sc_out_w': sc_out_w[l], 'na_rpb': na_rpb[l], 'na_out_w': na_out_w[l], 'w_o': w_o[l],
            'norm2_g': norm2_g[l], 'ffn_w_gate': ffn_w_gate[l], 'ffn_w_up': ffn_w_up[l],
            'ffn_w_down': ffn_w_down[l],
        }
        xp, k_l, v_l = trunk_layer(xp, cond_ctx, p, attend_context)
        new_k_list.append(k_l)
        new_v_list.append(v_l)
        attend_l = functools.partial(attend_latent, k_ctx=cache_k[:, l], v_ctx=cache_v[:, l])
        xs, _, _ = trunk_layer(xs, cond_lat, p, attend_l)
    y_prompt = rmsnorm(xp, final_g)
    y_sample = rmsnorm(xs, final_g)
    new_k = jnp.stack(new_k_list, axis=1)
    new_v = jnp.stack(new_v_list, axis=1)
    return (y_prompt, y_sample, new_k, new_v)
```

```python
import contextlib
import numpy as np
import concourse.bass as bass
import concourse.mybir as mybir
from concourse.bass_utils import run_bass_kernel_spmd

F32 = mybir.dt.float32
BF16 = mybir.dt.bfloat16
U8 = mybir.dt.uint8
ALU = mybir.AluOpType
AF = mybir.ActivationFunctionType

D = 1024
NCH = 8
DEPTH = 4
NH = 16
GW = 64
ROWS = 32
D_FF = 2816
NFF = 22
D_IN = 11264
EPS = 1e-6
ATT_SCALE = 0.125
NEG = -30000.0
LEFT = 320
UW = 1088
NSLOT = 3
SLOT_E = 4096
ARENA_BYTES = 31 * 1024

COMPUTE = ("pe", "act", "dve", "pool")
DMAQ = ("act", "pool", "sp")


class Buf:
    __slots__ = ("name", "last_w", "rd", "rd_dma", "rng", "ov", "excl")

    def __init__(self, name, rng=None, excl=False):
        self.excl = excl
        self.name = name
        self.last_w = None
        self.rd = {}
        self.rd_dma = []
        self.rng = rng
        self.ov = []


class Op:
    __slots__ = ("eng", "fn", "deps", "is_dma", "signal", "sem", "semval")

    def __init__(self, eng, fn, is_dma):
        self.eng = eng
        self.fn = fn
        self.is_dma = is_dma
        self.deps = []
        self.signal = is_dma
        self.sem = None
        self.semval = 0


class Sched:
    def __init__(self, nc, n_dma_sems=6, same_engine_sync=True):
        self.nc = nc
        self.streams = {e: [] for e in ("pe", "act", "dve", "pool", "sp")}
        self.n_dma_sems = n_dma_sems
        self.dma_hist = {e: [] for e in DMAQ}
        self.same_engine_sync = same_engine_sync
        self.final_ops = []
        self.arena_bufs = []
        self.plan = False

    def arena_buf(self, name, lo, hi):
        b = Buf(name, (lo, hi))
        if self.plan:
            return b
        for o in self.arena_bufs:
            if o.rng[0] < hi and lo < o.rng[1]:
                o.ov.append(b)
                b.ov.append(o)
        self.arena_bufs.append(b)
        return b

    def op(self, eng, fn, reads=(), writes=(), dma=False, final=False):
        if self.plan:
            return None
        o = Op(eng, fn, dma)
        deps = []
        for b in reads:
            if b.last_w is not None:
                deps.append(b.last_w)
            if b.excl:
                deps.extend(b.rd.values())
            for x in b.ov:
                if x.last_w is not None:
                    deps.append(x.last_w)
        for b in writes:
            if b.last_w is not None:
                deps.append(b.last_w)
            deps.extend(b.rd.values())
            deps.extend(b.rd_dma)
            for x in b.ov:
                if x.last_w is not None:
                    deps.append(x.last_w)
                deps.extend(x.rd.values())
                deps.extend(x.rd_dma)
        if dma:
            h = self.dma_hist[eng]
            if len(h) >= self.n_dma_sems:
                deps.append(h[-self.n_dma_sems])
            h.append(o)
        seen = set()
        for d in deps:
            if d is o or id(d) in seen:
                continue
            seen.add(id(d))
            if (not d.is_dma) and d.eng == eng and not dma:
                if eng == "pe" or not self.same_engine_sync:
                    continue
            o.deps.append(d)
            d.signal = True
        for b in writes:
            b.last_w = o
            b.rd = {}
            b.rd_dma = []
        for b in reads:
            if b.last_w is not o:
                if dma:
                    b.rd_dma.append(o)
                else:
                    b.rd[eng] = o
        self.streams[eng].append(o)
        if final:
            self.final_ops.append(o)
        return o

    def emit(self):
        nc = self.nc
        with contextlib.ExitStack() as st:
            sems = {e: st.enter_context(nc.semaphore("s_" + e)) for e in COMPUTE}
            dsems = {e: [st.enter_context(nc.semaphore("d_%s%d" % (e, i))) for i in range(self.n_dma_sems)]
                     for e in DMAQ}
            for e in COMPUTE:
                cnt = 0
                for o in self.streams[e]:
                    if o.is_dma:
                        continue
                    if o.signal:
                        cnt += 1
                        o.sem = sems[e]
                        o.semval = cnt
            for e in DMAQ:
                cnts = [0] * self.n_dma_sems
                i = 0
                for o in self.streams[e]:
                    if not o.is_dma:
                        continue
                    k = i % self.n_dma_sems
                    cnts[k] += 16
                    o.sem = dsems[e][k]
                    o.semval = cnts[k]
                    i += 1
            block = st.enter_context(nc.Block())

            def run(ename, eng):
                waited = {}
                for o in self.streams[ename]:
                    for d in o.deps:
                        key = id(d.sem)
                        if waited.get(key, 0) >= d.semval:
                            continue
                        eng.wait_ge(d.sem, d.semval)
                        waited[key] = d.semval
                    ins = o.fn(eng)
                    if o.signal:
                        ins.then_inc(o.sem, 16 if o.is_dma else 1)
                if ename == "sp":
                    for o in self.final_ops:
                        key = id(o.sem)
                        if waited.get(key, 0) >= o.semval:
                            continue
                        eng.wait_ge(o.sem, o.semval)
                        waited[key] = o.semval

            @block.tensor
            def _(e):
                run("pe", e)

            @block.scalar
            def _(e):
                run("act", e)

            @block.vector
            def _(e):
                run("dve", e)

            @block.gpsimd
            def _(e):
                run("pool", e)

            @block.sync
            def _(e):
                run("sp", e)


class StopBuild(Exception):
    pass


STOP = None
DUMPS = []


def checkpoint(name):
    if STOP == name:
        raise StopBuild(name)


class Rot:
    def __init__(self, items):
        self.items = items
        self.i = 0

    def next(self):
        x = self.items[self.i % len(self.items)]
        self.i += 1
        return x


def rs_of(r):
    return min(max(r - 4, 0), ROWS - 8)


def build_program(depth=DEPTH, do_S=True, do_P=True, s_tiles=4):
    nc = bass.Bass("TRN2", target_bir_lowering=False)
    S = Sched(nc)

    def din(name, shape, dt=F32):
        return nc.dram_tensor(name, list(shape), dt, kind="ExternalInput").ap()

    def dout(name, shape):
        return nc.dram_tensor(name, list(shape), F32, kind="ExternalOutput").ap()

    xs_d = din("xs", [2048, D])
    xp_d = din("xp", [512, D])
    ck_d = din("ck", [DEPTH, NH, 256, 64])
    cv_d = din("cv", [DEPTH, NH, 256, 64])
    condT_d = din("condT", [128, 8, 2])
    bada_d = din("bada", [128, DEPTH, 48])
    vec8_d = din("vec8", [128, DEPTH, 5, 8])
    fing_d = din("fing", [128, 8])
    dww_d = din("dww", [128, DEPTH, 8, 31])
    scw_d = din("scw", [128, DEPTH, 8, 3])
    rpbR_d = din("rpbR", [DEPTH, NH, 128, 15])
    cmask_d = din("cmask", [128, 64])
    identf_d = din("identf", [128, 128])
    jrev_d = din("jrev", [128, 128])
    w_ada_d = din("w_ada", [DEPTH, D, 6 * D])
    w_in_d = din("w_in", [DEPTH, D, D_IN])
    sq_d = {n: din(n, [DEPTH, D, D]) for n in ("conv_pw_w", "sc_out_w", "na_out_w", "w_o")}
    wg_d = din("ffn_w_gate", [DEPTH, D, D_FF])
    wu_d = din("ffn_w_up", [DEPTH, D, D_FF])
    wd_d = din("ffn_w_down", [DEPTH, D_FF, D])
    ys_d = dout("ys", [2048, D])
    yp_d = dout("yp", [512, D])
    nk_d = dout("nk", [2, DEPTH, NH, 256, 64])
    nv_d = dout("nv", [2, DEPTH, NH, 256, 64])

    blocks = {}
    order_layer = []

    def add_block(bid, kc, nb, casts):
        blocks[bid] = dict(idx=len(blocks), kc=kc, nb=nb, casts=casts)

    for l in range(depth):
        wa = w_ada_d[l].rearrange("(kc p) n -> p kc n", p=128)
        wi = w_in_d[l].rearrange("(kc p) n -> p kc n", p=128)
        for b in range(12):
            add_block((l, "ada", b), 8, 512, [(0, 512, wa[:, :, 512 * b:512 * b + 512])])
        for b in range(22):
            add_block((l, "win", b), 8, 512, [(0, 512, wi[:, :, 512 * b:512 * b + 512])])
        for n, key in (("conv_pw_w", "pw"), ("sc_out_w", "sco"), ("na_out_w", "nao"), ("w_o", "wo")):
            ws_ = sq_d[n][l].rearrange("(kc p) n -> p kc n", p=128)
            for b in range(2):
                add_block((l, key, b), 8, 512, [(0, 512, ws_[:, :, 512 * b:512 * b + 512])])
        g_ = wg_d[l].rearrange("(kc p) n -> p kc n", p=128)
        u_ = wu_d[l].rearrange("(kc p) n -> p kc n", p=128)
        for b in range(11):
            add_block((l, "ffn", b), 8, 512, [(0, 256, g_[:, :, 256 * b:256 * b + 256]),
                                             (256, 256, u_[:, :, 256 * b:256 * b + 256])])
        d_ = wd_d[l].rearrange("(kc p) n -> p kc n", p=128)
        for b in range(8):
            add_block((l, "dn", b), 22, 128, [(0, 128, d_[:, :, 128 * b:128 * b + 128])])
    wsc_d = nc.dram_tensor("wsc", [len(blocks), 128, SLOT_E], BF16, kind="Internal").ap()
    Gs_d = nc.dram_tensor("Gs", [DEPTH, 128, 8 * 15 * 64], BF16, kind="Internal").ap()
    KTs_d = nc.dram_tensor("KTs", [DEPTH, 128, 8 * 256], BF16, kind="Internal").ap()
    bGs = [Buf("Gs%d" % l) for l in range(DEPTH)]
    bKTs = [Buf("KTs%d" % l) for l in range(DEPTH)]
    bscr = {bid: Buf("scr%d" % blk["idx"]) for bid, blk in blocks.items()}

    def tile_blocks(l):
        seq = []
        for hg in range(2):
            seq += [(l, "win", 4 + hg), (l, "win", 6 + hg), (l, "win", 8 + hg)]
        for h in range(2):
            seq += [(l, "nao", h), (l, "win", 18 + h)]
        for h in range(2):
            seq += [(l, "win", 0 + h), (l, "win", 2 + h)]
        for h in range(2):
            seq += [(l, "pw", h), (l, "win", 16 + h)]
        for h in range(2):
            seq += [(l, "win", 12 + h), (l, "win", 14 + h)]
        seq += [(l, "win", 10), (l, "win", 11)]
        for h in range(2):
            seq += [(l, "sco", h), (l, "win", 20 + h)]
        seq += [(l, "wo", 0), (l, "wo", 1)]
        seq += [(l, "ffn", b) for b in range(11)]
        seq += [(l, "dn", b) for b in range(8)]
        return seq

    def ada_blocks(l):
        return [(l, "ada", b) for b in range(12)]

    with contextlib.ExitStack() as st:
        def sb(name, shape, dt):
            return st.enter_context(nc.sbuf_tensor(name, list(shape), dt))

        xT = sb("xT", [128, 8, 2048], F32)
        uw = sb("uw", [128, 8, UW], BF16)
        ring = [sb("ring%d" % i, [128, SLOT_E], BF16) for i in range(NSLOT)]
        G = sb("G", [128, 8, 15, 64], BF16)
        ctxKT = sb("ctxKT", [128, 8, 256], BF16)
        ctxV = sb("ctxV", [128, 2, 1024], BF16)
        yat = sb("yat", [128, 8, 512], BF16)
        mixed = sb("mixed", [128, 8, 512], BF16)
        arena_t = sb("arena", [128, ARENA_BYTES], U8)
        tmpf_t = [sb("tmpf%d" % i, [128, 576], F32) for i in range(4)]
        tmpb_t = [sb("tmpb%d" % i, [128, 576], BF16) for i in range(4)]
        rsd_t = [sb("rsd%d" % i, [128, 512], F32) for i in range(2)]
        modt = sb("modt", [128, DEPTH, 48, 2], F32)
        A1t = sb("A1t", [128, DEPTH, 8, 2], F32)
        A2t = sb("A2t", [128, DEPTH, 8, 2], F32)
        condT = sb("condT_s", [128, 8, 2], F32)
        condS = sb("condS", [128, 8, 2], BF16)
        bada = sb("bada_s", [128, DEPTH, 48], F32)
        vec8 = sb("vec8_s", [128, DEPTH, 5, 8], F32)
        fing = sb("fing_s", [128, 8], F32)
        dww = sb("dww_s", [128, DEPTH, 8, 31], F32)
        scw = sb("scw_s", [128, DEPTH, 8, 3], F32)
        cmask = sb("cmask_s", [128, 64], F32)
        identf = sb("identf_s", [128, 128], F32)
        identb = sb("identb_s", [128, 128], BF16)
        jrevf = sb("jrevf_s", [128, 128], F32)
        jrevb = sb("jrevb_s", [128, 128], BF16)
        onesb = sb("onesb", [128, 128], BF16)
        onesbd = sb("onesbd", [128, 128], BF16)
        ps = [st.enter_context(nc.psum_tensor("ps%d" % i, [128, 512], F32)) for i in range(8)]

        bps = [Buf("ps%d" % i, excl=True) for i in range(8)]
        bx = [[Buf("x%d_%d" % (t, j)) for j in range(8)] for t in range(4)]
        bu = [Buf("u%d" % k) for k in range(8)]
        bslot = [Buf("slot%d" % i) for i in range(NSLOT)]
        bG, bctxK, bctxV = Buf("G"), Buf("ctxK"), Buf("ctxV")
        byat = [Buf("yat%d" % j) for j in range(8)]
        bmix = [Buf("mix%d" % j) for j in range(8)]
        bconst = Buf("const")
        bmod = Buf("mod")
        tmpf = Rot([(t[:], Buf("tmpf%d" % i)) for i, t in enumerate(tmpf_t)])
        tmpb = Rot([(t[:], Buf("tmpb%d" % i)) for i, t in enumerate(tmpb_t)])
        rsd = Rot([(t[:], Buf("rsd%d" % i)) for i, t in enumerate(rsd_t)])
        bank_main = Rot([0, 1])
        bank_aux = Rot([2, 3])
        bank_s = Rot([4, 5])
        BN, BD = 6, 7

        def MM(out, lhsT, rhs, start, stop, reads, writes, skip=False):
            S.op("pe", lambda e: e.matmul(out, lhsT=lhsT, rhs=rhs, start=start, stop=stop, skip_group_check=skip),
                 reads=reads, writes=writes)

        def TR(out, in_, ident, reads, writes):
            S.op("pe", lambda e: e.transpose(out=out, in_=in_, identity=ident), reads=reads, writes=writes)

        def ACT(out, in_, func, reads, writes, bias=None, scale=None):
            kw = {}
            if bias is not None:
                kw["bias"] = bias
            if scale is not None:
                kw["scale"] = scale
            S.op("act", lambda e: e.activation(out=out, in_=in_, func=func, **kw), reads=reads, writes=writes)

        def TT(out, in0, in1, op, reads, writes, eng="dve"):
            S.op(eng, lambda e: e.tensor_tensor(out=out, in0=in0, in1=in1, op=op), reads=reads, writes=writes)

        def TS(out, in0, s1, s2, op0, op1, reads, writes, eng="dve"):
            if op1 is None:
                S.op(eng, lambda e: e.tensor_scalar(out=out, in0=in0, scalar1=s1, scalar2=None, op0=op0),
                     reads=reads, writes=writes)
            else:
                S.op(eng, lambda e: e.tensor_scalar(out=out, in0=in0, scalar1=s1, scalar2=s2, op0=op0, op1=op1),
                     reads=reads, writes=writes)

        def STT(out, in0, scalar, in1, op0, op1, reads, writes):
            S.op("dve", lambda e: e.scalar_tensor_tensor(out=out, in0=in0, scalar=scalar, in1=in1, op0=op0, op1=op1),
                 reads=reads, writes=writes)

        def CPY(eng, out, in_, reads, writes):
            if eng == "act":
                ACT(out, in_, AF.Identity, reads, writes)
            else:
                S.op(eng, lambda e: e.tensor_copy(out=out, in_=in_), reads=reads, writes=writes)

        def MEMSET(eng, ap, val, writes):
            S.op(eng, lambda e: e.memset(ap, val), writes=writes)

        def DMA(eng, out, in_, reads, writes, final=False):
            return S.op(eng, lambda e: e.dma_start(out=out, in_=in_), reads=reads, writes=writes, dma=True, final=final)

        class Arena:
            def __init__(self):
                self.off = 0

            def reset(self):
                self.off = 0

            def alloc(self, name, n, dt):
                esz = 4 if dt == F32 else 2
                nb = (n * esz + 31) // 32 * 32
                lo, hi = self.off, self.off + nb
                assert hi <= ARENA_BYTES, (name, hi)
                self.off = hi
                ap = arena_t[:, lo:lo + n * esz].bitcast(dt)
                return ap, S.arena_buf(name, lo, hi)

        arena = Arena()

        class WS:
            def __init__(self):
                self.seq = []
                self.pos = 0
                self.issued = 0
                self.cast_done = set()
                self.cast_queue = []

            def cast(self, bid):
                if bid in self.cast_done:
                    return
                self.cast_done.add(bid)
                blk = blocks[bid]
                dst = wsc_d[blk["idx"]][:, 0:blk["kc"] * blk["nb"]].rearrange("p (k n) -> p k n", n=blk["nb"])
                for (c0, ncol, src) in blk["casts"]:
                    DMA("pool", dst[:, :, c0:c0 + ncol], src, [], [bscr[bid]])

            def cast_some(self, n):
                if S.plan:
                    return
                while n > 0 and self.cast_queue:
                    self.cast(self.cast_queue.pop(0))
                    n -= 1

            def _issue(self, n):
                while self.issued <= n and self.issued < len(self.seq):
                    bid = self.seq[self.issued]
                    self.cast(bid)
                    blk = blocks[bid]
                    sl = self.issued % NSLOT
                    ne = blk["kc"] * blk["nb"]
                    DMA("sp", ring[sl][:, 0:ne], wsc_d[blk["idx"]][:, 0:ne], [bscr[bid]], [bslot[sl]])
                    self.issued += 1

            def get(self, bid):
                if S.plan:
                    self.seq.append(bid)
                    blk = blocks[bid]
                    return ring[0][:, 0:blk["kc"] * blk["nb"]].rearrange("p (k n) -> p k n", n=blk["nb"]), bslot[0]
                assert self.seq[self.pos] == bid, (self.pos, self.seq[self.pos], bid)
                self._issue(self.pos + NSLOT - 2)
                sl = self.pos % NSLOT
                self.pos += 1
                blk = blocks[bid]
                view = ring[sl][:, 0:blk["kc"] * blk["nb"]].rearrange("p (k n) -> p k n", n=blk["nb"])
                return view, bslot[sl]

        ws = WS()
        wsh = [ws]

        def emit_consts():
            for dst, src in ((condT, condT_d), (bada, bada_d), (vec8, vec8_d), (fing, fing_d), (dww, dww_d),
                             (scw, scw_d), (cmask, cmask_d), (identf, identf_d), (jrevf, jrev_d)):
                DMA("sp", dst[:], src, [], [bconst])
            CPY("dve", identb[:], identf[:], [bconst], [bconst])
            CPY("dve", jrevb[:], jrevf[:], [bconst], [bconst])
            MEMSET("dve", onesb[:], 1.0, [bconst])
            MEMSET("dve", onesbd[:], 1.0, [bconst])
            MEMSET("dve", onesbd[0:64, 64:128], 0.0, [bconst])
            MEMSET("dve", onesbd[64:128, 0:64], 0.0, [bconst])
            for k in range(8):
                MEMSET("dve", uw[:, k, :], 0.0, [bu[k]])
            ACT(condS[:], condT[:], AF.Silu, [bconst], [bconst])

        def proj(pb, ncols, wview, wcols, rhs_fn, nk, wbuf, rhs_bufs_fn, pcol0=0):
            for k in range(nk):
                MM(ps[pb][:, pcol0:pcol0 + ncols], wview[:, k, wcols], rhs_fn(k), k == 0, k == nk - 1,
                   [wbuf] + rhs_bufs_fn(k), [bps[pb]])

        def adaln(l):
            pb = bank_main.next()
            for b in range(12):
                W, wb = ws.get((l, "ada", b))
                for jj in range(4):
                    ch = 4 * b + jj
                    for k in range(8):
                        MM(ps[pb][:, 2 * ch:2 * ch + 2], W[:, k, jj * 128:(jj + 1) * 128], condS[:, k, :], k == 0, k == 7,
                           [wb, bconst], [bps[pb]])
            pv = ps[pb][:, 0:96].rearrange("p (a b) -> p a b", b=2)
            for c in range(2):
                TT(modt[:, l, :, c], pv[:, :, c], bada[:, l, :], ALU.add, [bps[pb], bconst], [bmod])
            for c in range(2):
                STT(A1t[:, l, :, c], modt[:, l, 8:16, c], 1.0, vec8[:, l, 0, :], ALU.add, ALU.mult, [bmod, bconst], [bmod])
                STT(A2t[:, l, :, c], modt[:, l, 32:40, c], 1.0, vec8[:, l, 1, :], ALU.add, ALU.mult, [bmod, bconst], [bmod])

        def rms_modulate(n, xsrc, xbufs, dst, dbufs, scale_ap, shift_ap):
            pb = bank_s.next()
            for j in range(8):
                sq, sqb = tmpb.next()
                ACT(sq[:, 0:n], xsrc(j), AF.Square, [xbufs(j)], [sqb])
                MM(ps[pb][:, 0:n], onesb[:], sq[:, 0:n], j == 0, j == 7, [bconst, sqb], [bps[pb]])
            ln_, lnb = tmpf.next()
            ACT(ln_[:, 0:n], ps[pb][:, 0:n], AF.Ln, [bps[pb]], [lnb], bias=EPS, scale=1.0 / D)
            rstd, rsb = rsd.next()
            ACT(rstd[:, 0:n], ln_[:, 0:n], AF.Exp, [lnb], [rsb], scale=-0.5)
            for j in range(8):
                t, tb = tmpf.next()
                TT(t[:, 0:n], xsrc(j), rstd[:, 0:n], ALU.mult, [xbufs(j), rsb], [tb])
                sh = shift_ap(j)
                ACT(dst(j), t[:, 0:n], AF.Identity, [tb, bmod, bconst], [dbufs(j)], bias=(sh if sh is not None else 0.0),
                    scale=scale_ap(j))

        def mix_out(l, cnd, ysrc, ybufs, wname, gate_blk, first):
            for half in range(2):
                W, wb = ws.get((l, wname, half))
                Wg, wgb = ws.get((l, "win", gate_blk + half))
                for jj in range(4):
                    j = 4 * half + jj
                    pb = bank_main.next()
                    proj(pb, 512, W, slice(jj * 128, (jj + 1) * 128), ysrc, 8, wb, lambda k: [ybufs(k)])
                    gb = bank_aux.next()
                    proj(gb, 512, Wg, slice(jj * 128, (jj + 1) * 128), lambda k: uw[:, k, LEFT:LEFT + 512], 8, wgb,
                         lambda k: [bu[k]])
                    g, gbuf = tmpf.next()
                    ACT(g[:, 0:512], ps[gb][:, 0:512], AF.Sigmoid, [bps[gb]], [gbuf])
                    if first:
                        TT(mixed[:, j, :], ps[pb][:, 0:512], g[:, 0:512], ALU.mult, [bps[pb], gbuf], [bmix[j]])
                    else:
                        t2, t2b = tmpf.next()
                        TT(t2[:, 0:512], ps[pb][:, 0:512], g[:, 0:512], ALU.mult, [bps[pb], gbuf], [t2b])
                        TT(mixed[:, j, :], mixed[:, j, :], t2[:, 0:512], ALU.add, [bmix[j], t2b], [bmix[j]])

        def load_ctxv(l):
            for c in range(2):
                DMA("pool", ctxV[:, c, :].rearrange("p (h d) -> p h d", d=64),
                    cv_d[l, :, c * 128:(c + 1) * 128, :].rearrange("h s d -> s h d"), [], [bctxV])

        def tables_pre():
            for l in reversed(range(depth)):
                arena.reset()
                hst = [arena.alloc("hstf%d" % i, 960, F32) for i in range(2)]
                kstf = [arena.alloc("kstf%d" % c, 16 * 64, F32) for c in range(2)]
                for hp in range(8):
                    h_, hb_ = hst[hp % 2]
                    for s in range(2):
                        src = bass.AP(rpbR_d.tensor, ((l * NH) + 2 * hp + s) * 15 * 128, [[15, 64], [1, 960]])
                        DMA("act", h_[64 * s:64 * s + 64, :], src, [], [hb_])
                    cm = cmask[:]
                    for w0 in (0, 32):
                        pb = bank_aux.next()
                        MM(ps[pb][:, 0:480], jrevf[:], h_[:, w0 * 15:(w0 + 32) * 15], True, True, [bconst, hb_], [bps[pb]])
                        cmb = bass.AP(cm.tensor, cm.offset + w0, [cm.ap[0], [0, 15], [1, 32]])
                        TT(G[:, hp, :, w0:w0 + 32], ps[pb][:, 0:480].rearrange("p (w e) -> p e w", e=15), cmb, ALU.add,
                           [bps[pb], bconst], [bG])
                for c in range(2):
                    DMA("act", kstf[c][0].rearrange("p (h d) -> p h d", d=64),
                        ck_d[l, :, c * 128:(c + 1) * 128, :].rearrange("h s d -> s h d"), [], [kstf[c][1]])
                for hp in range(8):
                    pb = bank_aux.next()
                    for c in range(2):
                        TR(ps[pb][:, c * 128:(c + 1) * 128], kstf[c][0][:, 2 * hp * 64:(2 * hp + 2) * 64], identf[:],
                           [kstf[c][1], bconst], [bps[pb]])
                    CPY("dve", ctxKT[:, hp, :], ps[pb][:, 0:256], [bps[pb]], [bctxK])
                if l > 0:
                    DMA("act", Gs_d[l], G[:].rearrange("p a b c -> p (a b c)"), [bG], [bGs[l]])
                    DMA("act", KTs_d[l], ctxKT[:].rearrange("p a b -> p (a b)"), [bctxK], [bKTs[l]])
            load_ctxv(0)

        def build_layer_tables(l):
            if l == 0:
                return
            DMA("sp", G[:].rearrange("p a b c -> p (a b c)"), Gs_d[l], [bGs[l]], [bG])
            DMA("sp", ctxKT[:].rearrange("p a b -> p (a b)"), KTs_d[l], [bKTs[l]], [bctxK])
            load_ctxv(l)

        def norm1(l, kind, ti):
            cnd = 0 if kind == "S" else 1
            own0 = 512 * ti
            A1 = lambda j: A1t[:, l, j, cnd:cnd + 1]
            mod = lambda idx, j: modt[:, l, idx * 8 + j, cnd:cnd + 1]
            if kind == "S" and ti > 0:
                for k in range(8):
                    CPY("act", uw[:, k, LEFT - 256:LEFT], uw[:, k, LEFT + 256:LEFT + 512], [bu[k]], [bu[k]])
            ranges = [(ti, own0, 512, LEFT)]
            if kind == "S" and ti < 3:
                ranges.append((ti + 1, own0 + 512, 256, LEFT + 512))
            for (tsrc, xc, n, uc) in ranges:
                rms_modulate(n, lambda j: xT[:, j, xc:xc + n], lambda j: bx[tsrc][j],
                             lambda j: uw[:, j, uc:uc + n], lambda j: bu[j], A1, lambda j: mod(0, j))

        def tile_prog(l, kind, ti, is_last, hoist):
            cnd = 0 if kind == "S" else 1
            own0 = 512 * ti
            xb_own = bx[ti]
            if kind == "S":
                r0 = 8 * ti
                kmin, kmax = rs_of(r0), rs_of(r0 + 7) + 7
                nrows = kmax - kmin + 1
                nseg, L = 1, 512
            else:
                nseg, L = 2, 256
            A1 = lambda j: A1t[:, l, j, cnd:cnd + 1]
            A2 = lambda j: A2t[:, l, j, cnd:cnd + 1]
            mod = lambda idx, j: modt[:, l, idx * 8 + j, cnd:cnd + 1]
            uown = lambda k: uw[:, k, LEFT:LEFT + 512]
            ubuf = lambda k: [bu[k]]

            if STOP == "norm1":
                DUMPS.append(("uw", uw[:, :, LEFT:LEFT + 512], bu))
                DUMPS.append(("modt", modt[:, 0], [bmod]))
                DUMPS.append(("A1t", A1t[:, 0], [bmod]))
                DUMPS.append(("xT", xT[:, :, 0:512], bx[0]))
            checkpoint("norm1")
            for hg in range(2):
                arena.reset()
                KT = [arena.alloc("KT%d" % jj, UW, BF16) for jj in range(4)]
                QT = [arena.alloc("QT%d" % jj, 512, BF16) for jj in range(4)]
                VA = [arena.alloc("VA%d" % m, 512, BF16) for m in range(8)]
                VS = [arena.alloc("VS%d" % m, 512, BF16) for m in range(8)]
                W, wb = ws.get((l, "win", 4 + hg))
                for jj in range(4):
                    pb = bank_main.next()
                    proj(pb, 512, W, slice(jj * 128, (jj + 1) * 128), uown, 8, wb, ubuf)
                    CPY("act", QT[jj][0], ps[pb][:, 0:512], [bps[pb]], [QT[jj][1]])
                checkpoint("q")
                W, wb = ws.get((l, "win", 6 + hg))
                if kind == "S":
                    c0, c1 = LEFT + (kmin - r0) * 64, LEFT + (kmax + 1 - r0) * 64
                else:
                    c0, c1 = LEFT, LEFT + 512
                pieces = []
                c = c0
                while c < c1:
                    n = min(512, c1 - c)
                    pieces.append((c, n))
                    c += n
                for jj in range(4):
                    for (c, n) in pieces:
                        pb = bank_main.next()
                        proj(pb, n, W, slice(jj * 128, (jj + 1) * 128), lambda k, c=c, n=n: uw[:, k, c:c + n], 8, wb, ubuf)
                        CPY("dve", KT[jj][0][:, c:c + n], ps[pb][:, 0:n], [bps[pb]], [KT[jj][1]])
                checkpoint("k")
                if kind == "P":
                    for tc in range(4):
                        pb = bank_aux.next()
                        for k in range(8):
                            MM(ps[pb][:, 0:512], uw[:, k, LEFT + 128 * tc:LEFT + 128 * tc + 128], W[:, k, :], k == 0, k == 7,
                               [wb, bu[k]], [bps[pb]])
                        stg, stb = tmpf.next()
                        CPY("act", stg[:, 0:512], ps[pb][:, 0:512], [bps[pb]], [stb])
                        dst = nk_d[tc // 2, l, 8 * hg:8 * hg + 8, (tc % 2) * 128:(tc % 2) * 128 + 128, :].rearrange("h s d -> s h d")
                        DMA("pool", dst, stg[:, 0:512].rearrange("p (h d) -> p h d", d=64), [stb], [], final=True)
                checkpoint("knk")
                W, wb = ws.get((l, "win", 8 + hg))
                vjobs = []
                if kind == "S":
                    nA = (nrows + 1) // 2
                    for m in range(nA):
                        vjobs.append((VA[m], LEFT + (kmin + 2 * m - r0) * 64, None))
                else:
                    for tc in range(4):
                        vjobs.append((VA[tc], LEFT + 128 * tc, tc))
                for (vdst, col, tc) in vjobs:
                    pb = bank_main.next()
                    for k in range(8):
                        MM(ps[pb][:, 0:512], uw[:, k, col:col + 128], W[:, k, :], k == 0, k == 7, [wb, bu[k]], [bps[pb]])
                    CPY("dve", vdst[0], ps[pb][:, 0:512], [bps[pb]], [vdst[1]])
                    if tc is None:
                        m_ = VA.index(vdst)
                        CPY("act", VS[m_][0][0:64, :], vdst[0][64:128, :], [vdst[1]], [VS[m_][1]])
                        CPY("dve", VS[m_][0][64:128, :], vdst[0][0:64, :], [vdst[1]], [VS[m_][1]])
                    if tc is not None:
                        stg, stb = tmpf.next()
                        CPY("act", stg[:, 0:512], ps[pb][:, 0:512], [bps[pb]], [stb])
                        dst = nv_d[tc // 2, l, 8 * hg:8 * hg + 8, (tc % 2) * 128:(tc % 2) * 128 + 128, :].rearrange("h s d -> s h d")
                        DMA("pool", dst, stg[:, 0:512].rearrange("p (h d) -> p h d", d=64), [stb], [], final=True)

                checkpoint("qkv")
                jobs = []
                for jj in range(4):
                    hp = 4 * hg + jj
                    kt, ktb = KT[jj]
                    qt, qtb = QT[jj]
                    first = len(jobs)
                    if kind == "S":
                        for q in range(nrows):
                            kr = kmin + q
                            rr = [r for r in range(r0, r0 + 8) if rs_of(r) <= kr <= rs_of(r) + 7]
                            ra, rb = rr[0], rr[-1]
                            nr = rb - ra + 1
                            nq = 64 * nr
                            qc = (ra - r0) * 64
                            kcol = LEFT + (kr - r0) * 64
                            e0 = ra - kr + 7
                            if q % 2 == 0:
                                vA, vB = VA[q // 2], VS[q // 2]
                            else:
                                vA, vB = VS[(q - 1) // 2], VA[(q - 1) // 2]
                            jobs.append(dict(kind="loc", hp=hp, jj=jj, nq=nq, qc=qc, kcol=kcol, e0=e0, nr=nr, vA=vA, vB=vB,
                                             kt=kt, ktb=ktb, qt=qt, qtb=qtb))
                        for s_ in range(2):
                            for c in range(2):
                                jobs.append(dict(kind="den", hp=hp, jj=jj, s=s_, nq=512, qc=0, qt=qt, qtb=qtb,
                                                 klhs=ctxKT[64 * s_:64 * s_ + 64, hp, c * 128:(c + 1) * 128], kbuf=bctxK,
                                                 vlhs=ctxV[:, c, (2 * hp + s_) * 64:(2 * hp + s_) * 64 + 64], vbuf=bctxV))
                    else:
                        for sq in range(2):
                            for s_ in range(2):
                                for c in range(2):
                                    kc0 = LEFT + sq * 256 + c * 128
                                    jobs.append(dict(kind="den", hp=hp, jj=jj, s=s_, nq=256, qc=sq * 256, qt=qt, qtb=qtb,
                                                     klhs=kt[64 * s_:64 * s_ + 64, kc0:kc0 + 128], kbuf=ktb,
                                                     vlhs=VA[2 * sq + c][0][:, jj * 128 + 64 * s_:jj * 128 + 64 * s_ + 64],
                                                     vbuf=VA[2 * sq + c][1]))
                    jobs[first]["first"] = True
                    jobs[-1]["last"] = True
                s_banks = Rot([4, 5, 0, 1])
                nd_pairs = Rot([(6, 7), (2, 3)])
                cur_nd = {}

                def emitS(jb):
                    if jb.get("first"):
                        bn, bd = nd_pairs.next()
                        cur_nd[jb["hp"]] = (bn, bd)
                        MEMSET("dve", ps[bn][:], 0.0, [bps[bn]])
                        MEMSET("dve", ps[bd][:], 0.0, [bps[bd]])
                    sbk = s_banks.next()
                    jb["sbk"] = sbk
                    nq, qc, qt, qtb = jb["nq"], jb["qc"], jb["qt"], jb["qtb"]
                    if jb["kind"] == "loc":
                        for s_ in range(2):
                            MM(ps[sbk][64 * s_:64 * s_ + 64, 0:nq], jb["kt"][64 * s_:64 * s_ + 64, jb["kcol"]:jb["kcol"] + 64],
                               qt[64 * s_:64 * s_ + 64, qc:qc + nq], True, True, [jb["ktb"], qtb], [bps[sbk]])
                    else:
                        s_ = jb["s"]
                        MM(ps[sbk][:, 0:nq], jb["klhs"], qt[64 * s_:64 * s_ + 64, qc:qc + nq], True, True, [jb["kbuf"], qtb], [bps[sbk]])

                def emitE(jb):
                    sbk, nq = jb["sbk"], jb["nq"]
                    p, pbuf = tmpb.next()
                    if jb["kind"] == "loc":
                        t, tb = tmpf.next()
                        STT(t[:, 0:nq], ps[sbk][:, 0:nq], ATT_SCALE,
                            G[:, jb["hp"], jb["e0"]:jb["e0"] + jb["nr"], :].rearrange("p a b -> p (a b)"),
                            ALU.mult, ALU.add, [bps[sbk], bG], [tb])
                        ACT(p[:, 0:nq], t[:, 0:nq], AF.Exp, [tb], [pbuf])
                    else:
                        ACT(p[:, 0:nq], ps[sbk][:, 0:nq], AF.Exp, [bps[sbk]], [pbuf], scale=ATT_SCALE)
                    jb["p"] = (p, pbuf)

                def emitPV(jb):
                    bn, bd = cur_nd[jb["hp"]]
                    nq, qc, jj = jb["nq"], jb["qc"], jb["jj"]
                    p, pbuf = jb["p"]
                    if jb["kind"] == "loc":
                        for s_, v in ((0, jb["vA"]), (1, jb["vB"])):
                            MM(ps[bn][64 * s_:64 * s_ + 64, qc:qc + nq],
                               v[0][64 * s_:64 * s_ + 64, jj * 128 + 64 * s_:jj * 128 + 64 * s_ + 64],
                               p[64 * s_:64 * s_ + 64, 0:nq], False, True, [v[1], pbuf], [bps[bn]], skip=True)
                        MM(ps[bd][:, qc:qc + nq], onesbd[:], p[:, 0:nq], False, True, [bconst, pbuf], [bps[bd]], skip=True)
                    else:
                        s_ = jb["s"]
                        MM(ps[bn][64 * s_:64 * s_ + 64, qc:qc + nq], jb["vlhs"], p[:, 0:nq], False, True, [jb["vbuf"], pbuf],
                           [bps[bn]], skip=True)
                        MM(ps[bd][64 * s_:64 * s_ + 64, qc:qc + nq], onesb[:, 0:64], p[:, 0:nq], False, True, [bconst, pbuf],
                           [bps[bd]], skip=True)
                    if jb.get("last"):
                        hp = jb["hp"]
                        lnd, lndb = tmpf.next()
                        ACT(lnd[:, 0:512], ps[bd][:, 0:512], AF.Ln, [bps[bd]], [lndb])
                        rec, recb = tmpf.next()
                        ACT(rec[:, 0:512], lnd[:, 0:512], AF.Exp, [lndb], [recb], scale=-1.0)
                        TT(yat[:, hp, :], ps[bn][:, 0:512], rec[:, 0:512], ALU.mult, [bps[bn], recb], [byat[hp]])

                LA = 3
                for i in range(min(LA, len(jobs))):
                    emitS(jobs[i])
                for i in range(len(jobs)):
                    emitE(jobs[i])
                    if i + LA < len(jobs):
                        emitS(jobs[i + LA])
                    emitPV(jobs[i])

            if STOP == "attn":
                DUMPS.append(("yat", yat[:], byat))
            checkpoint("attn")
            mix_out(l, cnd, lambda k: yat[:, k, :], lambda k: byat[k], "nao", 18, True)
            if STOP in ("nao", "conv", "sc"):
                DUMPS.append(("mixed", mixed[:], bmix))
            checkpoint("nao")

            arena.reset()
            SEG = L + 30
            agl = [arena.alloc("agl%d" % j, 576, BF16) for j in range(8)]
            hb = [(agl[j][0][:, 0:512], agl[j][1]) for j in range(8)]
            diag = [arena.alloc("diag%d" % i, 31 * 128, BF16) for i in range(2)]
            mu, mub = arena.alloc("mu", 512, F32)
            rstd, rstdb = arena.alloc("rstd", 512, F32)
            murs, mursb = arena.alloc("murs", 512, F32)
            sides = []
            if kind == "S":
                if ti > 0:
                    sides.append((LEFT - 15, 0))
                if ti < 3:
                    sides.append((LEFT + 512, 15 + 512))
            for half in range(2):
                Wa, wab = ws.get((l, "win", 0 + half))
                Wg, wgb = ws.get((l, "win", 2 + half))
                for jj in range(4):
                    j = 4 * half + jj
                    cs = slice(jj * 128, (jj + 1) * 128)
                    a_, ab_ = agl[j]
                    av = a_[:, 0:nseg * SEG].rearrange("p (s c) -> p s c", c=SEG)
                    pa = bank_main.next()
                    proj(pa, 512, Wa, cs, uown, 8, wab, ubuf)
                    pg = bank_aux.next()
                    proj(pg, 512, Wg, cs, uown, 8, wgb, ubuf)
                    sg, sgb = tmpf.next()
                    ACT(sg[:, 0:512], ps[pg][:, 0:512], AF.Sigmoid, [bps[pg]], [sgb])
                    MEMSET("dve", av[:, :, 0:15], 0.0, [ab_])
                    MEMSET("dve", av[:, :, 15 + L:30 + L], 0.0, [ab_])
                    TT(av[:, :, 15:15 + L], ps[pa][:, 0:512].rearrange("p (s c) -> p s c", c=L),
                       sg[:, 0:512].rearrange("p (s c) -> p s c", c=L), ALU.mult, [bps[pa], sgb], [ab_])
                    for (ucol, acol) in sides:
                        ph = bank_s.next()
                        proj(ph, 15, Wa, cs, lambda k, ucol=ucol: uw[:, k, ucol:ucol + 15], 8, wab, ubuf, pcol0=0)
                        proj(ph, 15, Wg, cs, lambda k, ucol=ucol: uw[:, k, ucol:ucol + 15], 8, wgb, ubuf, pcol0=32)
                        sh_, shb = tmpf.next()
                        ACT(sh_[:, 0:15], ps[ph][:, 32:47], AF.Sigmoid, [bps[ph]], [shb])
                        TT(a_[:, acol:acol + 15], ps[ph][:, 0:15], sh_[:, 0:15], ALU.mult, [bps[ph], shb], [ab_])
            idb_ = identb[:]
            for j in range(8):
                a_, ab_ = agl[j]
                av = a_[:, 0:nseg * SEG].rearrange("p (s c) -> p s c", c=SEG)
                dg, dgb = diag[j % 2]
                dgv = dg.rearrange("p (k m) -> p k m", m=128)
                wj = dww[:, l, j, :]
                TT(dgv, bass.AP(idb_.tensor, idb_.offset, [idb_.ap[0], [0, 31], [1, 128]]),
                   bass.AP(wj.tensor, wj.offset, [wj.ap[0], [1, 31], [0, 128]]), ALU.mult, [bconst], [dgb])
                pb = bank_main.next()
                for k in range(31):
                    MM(ps[pb][:, 0:512].rearrange("p (s c) -> p s c", c=L), dgv[:, k, :], av[:, :, k:k + L], k == 0, k == 30,
                       [dgb, ab_], [bps[pb]])
                ACT(hb[j][0], ps[pb][:, 0:512], AF.Identity, [bps[pb], bconst], [hb[j][1]], bias=vec8[:, l, 2, j:j + 1])
            p0, p1 = bank_s.next(), bank_s.next()
            for j in range(8):
                sq, sqb = tmpb.next()
                ACT(sq[:, 0:512], hb[j][0], AF.Square, [hb[j][1]], [sqb])
                MM(ps[p0][:, 0:512], onesb[:], hb[j][0], j == 0, j == 7, [bconst, hb[j][1]], [bps[p0]])
                MM(ps[p1][:, 0:512], onesb[:], sq[:, 0:512], j == 0, j == 7, [bconst, sqb], [bps[p1]])
            ACT(mu, ps[p0][:, 0:512], AF.Identity, [bps[p0]], [mub], scale=1.0 / D)
            msq, msqb = tmpf.next()
            TT(msq[:, 0:512], mu, mu, ALU.mult, [mub], [msqb])
            var, varb = tmpf.next()
            STT(var[:, 0:512], ps[p1][:, 0:512], 1.0 / D, msq[:, 0:512], ALU.mult, ALU.subtract, [bps[p1], msqb], [varb])
            lnv, lnvb = tmpf.next()
            ACT(lnv[:, 0:512], var[:, 0:512], AF.Ln, [varb], [lnvb], bias=EPS)
            ACT(rstd, lnv[:, 0:512], AF.Exp, [lnvb], [rstdb], scale=-0.5)
            TT(murs, mu, rstd, ALU.mult, [mub, rstdb], [mursb])
            for j in range(8):
                t, tb = tmpf.next()
                TT(t[:, 0:512], hb[j][0], rstd, ALU.mult, [hb[j][1], rstdb], [tb])
                t2, t2b = tmpf.next()
                TT(t2[:, 0:512], t[:, 0:512], murs, ALU.subtract, [tb, mursb], [t2b])
                ACT(hb[j][0], t2[:, 0:512], AF.Silu, [t2b, bconst], [hb[j][1]], bias=vec8[:, l, 4, j:j + 1],
                    scale=vec8[:, l, 3, j:j + 1])
            mix_out(l, cnd, lambda k: hb[k][0], lambda k: hb[k][1], "pw", 16, False)
            checkpoint("conv")

            arena.reset()
            SEG2 = L + 2
            cx = [arena.alloc("cx%d" % j, 520, BF16) for j in range(8)]
            scy = [arena.alloc("scy%d" % j, 512, BF16) for j in range(8)]
            sides2 = []
            if kind == "S":
                if ti > 0:
                    sides2.append((LEFT - 1, 0))
                if ti < 3:
                    sides2.append((LEFT + 512, 1 + 512))
            for half in range(2):
                Wc, wcb = ws.get((l, "win", 12 + half))
                Wx, wxb = ws.get((l, "win", 14 + half))
                for jj in range(4):
                    j = 4 * half + jj
                    cs = slice(jj * 128, (jj + 1) * 128)
                    c_, cb_ = cx[j]
                    cvw = c_[:, 0:nseg * SEG2].rearrange("p (s c) -> p s c", c=SEG2)
                    pc_ = bank_main.next()
                    proj(pc_, 512, Wc, cs, uown, 8, wcb, ubuf)
                    px = bank_aux.next()
                    proj(px, 512, Wx, cs, uown, 8, wxb, ubuf)
                    t, tb = tmpf.next()
                    CPY("act", t[:, 0:512], ps[pc_][:, 0:512], [bps[pc_]], [tb])
                    MEMSET("dve", cvw[:, :, 0:1], 0.0, [cb_])
                    MEMSET("dve", cvw[:, :, 1 + L:2 + L], 0.0, [cb_])
                    TT(cvw[:, :, 1:1 + L], t[:, 0:512].rearrange("p (s c) -> p s c", c=L),
                       ps[px][:, 0:512].rearrange("p (s c) -> p s c", c=L), ALU.mult, [tb, bps[px]], [cb_])
                    for (ucol, ccol) in sides2:
                        ph = bank_s.next()
                        proj(ph, 1, Wc, cs, lambda k, ucol=ucol: uw[:, k, ucol:ucol + 1], 8, wcb, ubuf, pcol0=0)
                        proj(ph, 1, Wx, cs, lambda k, ucol=ucol: uw[:, k, ucol:ucol + 1], 8, wxb, ubuf, pcol0=8)
                        th, thb = tmpf.next()
                        CPY("act", th[:, 0:1], ps[ph][:, 0:1], [bps[ph]], [thb])
                        TT(c_[:, ccol:ccol + 1], th[:, 0:1], ps[ph][:, 8:9], ALU.mult, [thb, bps[ph]], [cb_])
            for half in range(2):
                Wb, wbb = ws.get((l, "win", 10 + half))
                for jj in range(4):
                    j = 4 * half + jj
                    c_, cb_ = cx[j]
                    cvw = c_[:, 0:nseg * SEG2].rearrange("p (s c) -> p s c", c=SEG2)
                    c3, c3b = tmpf.next()
                    c3v = c3[:, 0:512].rearrange("p (s c) -> p s c", c=L)
                    TS(c3v, cvw[:, :, 0:L], scw[:, l, j, 0:1], None, ALU.mult, None, [cb_, bconst], [c3b])
                    for k in (1, 2):
                        STT(c3v, cvw[:, :, k:k + L], scw[:, l, j, k:k + 1], c3v, ALU.mult, ALU.add, [cb_, bconst, c3b], [c3b])
                    pb = bank_main.next()
                    proj(pb, 512, Wb, slice(jj * 128, (jj + 1) * 128), uown, 8, wbb, ubuf)
                    TT(scy[j][0], ps[pb][:, 0:512], c3[:, 0:512], ALU.mult, [bps[pb], c3b], [scy[j][1]])
            mix_out(l, cnd, lambda k: scy[k][0], lambda k: scy[k][1], "sco", 20, False)
            checkpoint("sc")

            hoist()
            for half in range(2):
                W, wb = ws.get((l, "wo", half))
                for jj in range(4):
                    j = 4 * half + jj
                    pb = bank_main.next()
                    proj(pb, 512, W, slice(jj * 128, (jj + 1) * 128), lambda k: mixed[:, k, :], 8, wb, lambda k: [bmix[k]])
                    xo = xT[:, j, own0:own0 + 512]
                    STT(xo, ps[pb][:, 0:512], mod(2, j), xo, ALU.mult, ALU.add, [bps[pb], bmod, xb_own[j]], [xb_own[j]])

            if STOP in ("wo", "ffn"):
                DUMPS.append(("xT", xT[:, :, own0:own0 + 512], xb_own))
            checkpoint("wo")
            arena.reset()
            u2 = [arena.alloc("u2_%d" % k, 512, BF16) for k in range(8)]
            hm = [arena.alloc("hm%d" % c, 512, BF16) for c in range(NFF)]
            rms_modulate(512, lambda j: xT[:, j, own0:own0 + 512], lambda j: xb_own[j],
                         lambda j: u2[j][0], lambda j: u2[j][1], A2, lambda j: mod(3, j))
            for b in range(11):
                W, wb = ws.get((l, "ffn", b))
                for c2 in range(2):
                    hc = 2 * b + c2
                    pg = bank_main.next()
                    proj(pg, 512, W, slice(c2 * 128, (c2 + 1) * 128), lambda k: u2[k][0], 8, wb, lambda k: [u2[k][1]])
                    pu = bank_aux.next()
                    proj(pu, 512, W, slice(256 + c2 * 128, 256 + (c2 + 1) * 128), lambda k: u2[k][0], 8, wb, lambda k: [u2[k][1]])
                    t, tb = tmpf.next()
                    ACT(t[:, 0:512], ps[pg][:, 0:512], AF.Silu, [bps[pg]], [tb])
                    TT(hm[hc][0], t[:, 0:512], ps[pu][:, 0:512], ALU.mult, [tb, bps[pu]], [hm[hc][1]])
            for j in range(8):
                W, wb = ws.get((l, "dn", j))
                pb = bank_main.next()
                proj(pb, 512, W, slice(0, 128), lambda k: hm[k][0], NFF, wb, lambda k: [hm[k][1]])
                xo = xT[:, j, own0:own0 + 512]
                STT(xo, ps[pb][:, 0:512], mod(5, j), xo, ALU.mult, ALU.add, [bps[pb], bmod, xb_own[j]], [xb_own[j]])

            checkpoint("ffn")
            if is_last:
                arena.reset()
                yT = [arena.alloc("yT%d" % j, 512, F32) for j in range(8)]
                stg = [arena.alloc("stg%d" % i, 1024, F32) for i in range(2)]
                rms_modulate(512, lambda j: xT[:, j, own0:own0 + 512], lambda j: xb_own[j],
                             lambda j: yT[j][0], lambda j: yT[j][1], lambda j: fing[:, j:j + 1], lambda j: None)
                ydst = ys_d if kind == "S" else yp_d
                for tc in range(4):
                    sg_, sgb_ = stg[tc % 2]
                    for half in range(2):
                        pb = bank_main.next()
                        for q in range(4):
                            j = 4 * half + q
                            TR(ps[pb][:, q * 128:(q + 1) * 128], yT[j][0][:, tc * 128:(tc + 1) * 128], identf[:],
                               [yT[j][1], bconst], [bps[pb]])
                        CPY("act" if half == 0 else "dve", sg_[:, half * 512:(half + 1) * 512], ps[pb][:, 0:512], [bps[pb]], [sgb_])
                    DMA("pool", ydst[own0 + tc * 128:own0 + tc * 128 + 128, :], sg_, [sgb_], [], final=True)

        def load_x(src_d, ntc):
            arena.reset()
            stg = [arena.alloc("xin%d" % i, 1024, F32) for i in range(2)]
            for tc in range(ntc):
                sg_, sgb_ = stg[tc % 2]
                DMA("sp", sg_, src_d[tc * 128:(tc + 1) * 128, :], [], [sgb_])
                t = tc // 4
                for half in range(2):
                    pb = bank_main.next()
                    for q in range(4):
                        j = 4 * half + q
                        TR(ps[pb][:, q * 128:(q + 1) * 128], sg_[:, j * 128:(j + 1) * 128], identf[:], [sgb_, bconst], [bps[pb]])
                    CPY("act" if half == 0 else "dve", xT[:, 4 * half:4 * half + 4, tc * 128:(tc + 1) * 128],
                        ps[pb][:, 0:512].rearrange("p (a b) -> p a b", b=128), [bps[pb]], [bx[t][4 * half + q] for q in range(4)])

        per_tile_casts = (12 + 61 + s_tiles - 1) // s_tiles

        def emit_all():
            emit_consts()
            checkpoint("consts")
            if do_S:
                load_x(xs_d, min(16, 4 * (s_tiles + 1)))
                checkpoint("load")
                tables_pre()
                ws.cast_some(12 + 61)
                adaln(0)
                checkpoint("ada")
                build_layer_tables(0)
                checkpoint("tables")
                norm1(0, "S", 0)
                for l in range(depth):
                    for ti in range(s_tiles):
                        def hoist(l=l, ti=ti):
                            if ti + 1 < s_tiles:
                                norm1(l, "S", ti + 1)
                            elif l + 1 < depth:
                                adaln(l + 1)
                                build_layer_tables(l + 1)
                                norm1(l + 1, "S", 0)
                            elif do_P:
                                load_x(xp_d, 4)
                                norm1(0, "P", 0)
                        ws.cast_some(per_tile_casts)
                        tile_prog(l, "S", ti, l == depth - 1, hoist)
            if do_P:
                if not do_S:
                    ws.cast_some(12 + 61)
                    load_x(xp_d, 4)
                    checkpoint("load")
                    for l in range(depth):
                        adaln(l)
                    checkpoint("ada")
                    norm1(0, "P", 0)
                for l in range(depth):
                    if l > 0:
                        norm1(l, "P", 0)
                    tile_prog(l, "P", 0, l == depth - 1, lambda: None)

        S.plan = True
        try:
            emit_all()
        except StopBuild:
            pass
        seq = ws.seq
        S.plan = False
        S.arena_bufs = []
        ws = WS()
        ws.seq = seq
        seen = set()
        for bid in seq:
            if bid not in seen:
                seen.add(bid)
                ws.cast_queue.append(bid)
        for r_ in (tmpf, tmpb, rsd, bank_main, bank_aux, bank_s):
            r_.i = 0
        DUMPS.clear()
        try:
            emit_all()
            assert ws.pos == len(ws.seq), (ws.pos, len(ws.seq))
        except StopBuild as ex:
            print("STOPPED at", ex)
        for (name, ap, bufs) in DUMPS:
            dd = nc.dram_tensor("dbg_" + name, list(ap.shape), ap.dtype, kind="ExternalOutput").ap()
            DMA("sp", dd, ap, bufs, [], final=True)
        S.emit()
    return nc, S


def _fm(v):
    v = np.asarray(v, np.float32)
    lead = v.shape[:-1]
    r = v.reshape(lead + (8, 128))
    r = np.moveaxis(r, -1, 0)
    return np.ascontiguousarray(r)


_PROG = {}


def _host_consts():
    cols = np.arange(GW)
    cs = np.clip(cols - 8, 0, GW - 16)
    cm = np.full((128, 64), NEG, np.float32)
    for w in range(GW):
        for s in range(2):
            cm[64 * s + cs[w]:64 * s + cs[w] + 16, w] = 0.0
    jrev = np.zeros((128, 128), np.float32)
    for s in range(2):
        for c in range(64):
            jrev[64 * s + c, 64 * s + 63 - c] = 1.0
    return cm, np.eye(128, dtype=np.float32), jrev


def make_in_maps(inputs, depth=DEPTH):
    f32 = lambda a: np.ascontiguousarray(np.asarray(a, np.float32))
    x_prompt, x_sample = f32(inputs["x_prompt"]), f32(inputs["x_sample"])
    cache_k, cache_v = f32(inputs["cache_k"]), f32(inputs["cache_v"])
    c, c_ctx = f32(inputs["c"]), f32(inputs["c_ctx"])
    cm, ident, jrev = _host_consts()
    rpb = f32(inputs["na_rpb"])
    rpbR = np.zeros((DEPTH, NH, 128, 15), np.float32)
    rpbR[:, :, 48:79, :] = rpb[:, :, ::-1, ::-1].transpose(0, 1, 3, 2)
    bada = np.ascontiguousarray(np.asarray(inputs["b_ada"], np.float32).reshape(DEPTH, 48, 128).transpose(2, 0, 1))
    vec8 = np.stack([_fm(inputs[n]) for n in ("norm1_g", "norm2_g", "conv_dw_b", "conv_ln_g", "conv_ln_b")], axis=2)
    fing = _fm(inputs["final_g"])
    dww = np.ascontiguousarray(_fm(inputs["conv_dw_w"]).transpose(0, 1, 3, 2))
    scw = np.ascontiguousarray(_fm(inputs["sc_dw_w"]).transpose(0, 1, 3, 2))
    shared = dict(bada=bada, vec8=np.ascontiguousarray(vec8), fing=fing, dww=dww, scw=scw, rpbR=rpbR, cmask=cm, identf=ident,
                  jrev=jrev, w_ada=f32(inputs["w_ada"]), w_in=f32(inputs["w_in"]), conv_pw_w=f32(inputs["conv_pw_w"]),
                  sc_out_w=f32(inputs["sc_out_w"]), na_out_w=f32(inputs["na_out_w"]), w_o=f32(inputs["w_o"]),
                  ffn_w_gate=f32(inputs["ffn_w_gate"]), ffn_w_up=f32(inputs["ffn_w_up"]), ffn_w_down=f32(inputs["ffn_w_down"]))
    maps = []
    for core in range(8):
        cond = np.stack([c[core], c_ctx], axis=0)
        condT = np.ascontiguousarray(cond.reshape(2, 8, 128).transpose(2, 1, 0))
        m = dict(shared)
        m.update(xs=x_sample[core], xp=np.ascontiguousarray(x_prompt[2 * core:2 * core + 2].reshape(512, D)),
                 ck=cache_k[core], cv=cache_v[core], condT=condT)
        maps.append(m)
    return maps


def kernel(**inputs):
    if "nc" not in _PROG:
        _PROG["nc"], _ = build_program()
    nc = _PROG["nc"]
    maps = make_in_maps(inputs)
    res = run_bass_kernel_spmd(nc, maps, core_ids=list(range(8)))
    r = res.results
    y_sample = np.stack([r[i]["ys"] for i in range(8)], axis=0)
    y_prompt = np.concatenate([r[i]["yp"].reshape(2, 256, D) for i in range(8)], axis=0)
    new_k = np.concatenate([r[i]["nk"] for i in range(8)], axis=0)
    new_v = np.concatenate([r[i]["nv"] for i in range(8)], axis=0)
    return (y_prompt.astype(np.float32), y_sample.astype(np.float32), new_k.astype(np.float32), new_v.astype(np.float32))
```

```python
import contextlib
import numpy as np
import concourse.bass as bass
import concourse.mybir as mybir
from concourse.bass_utils import run_bass_kernel_spmd

F32 = mybir.dt.float32
BF16 = mybir.dt.bfloat16
U8 = mybir.dt.uint8
ALU = mybir.AluOpType
AF = mybir.ActivationFunctionType

D = 1024
NCH = 8
DEPTH = 4
NH = 16
GW = 64
ROWS = 32
D_FF = 2816
NFF = 22
D_IN = 11264
EPS = 1e-6
ATT_SCALE = 0.125
NEG = -30000.0
LEFT = 320
UW = 1088
NSLOT = 3
SLOT_E = 4096
ARENA_BYTES = 36 * 1024

COMPUTE = ("pe", "act", "dve", "pool")
DMAQ = ("act", "pool", "sp")


class Buf:
    __slots__ = ("name", "last_w", "rd", "rd_dma", "rng", "ov", "excl")

    def __init__(self, name, rng=None, excl=False):
        self.excl = excl
        self.name = name
        self.last_w = None
        self.rd = {}
        self.rd_dma = []
        self.rng = rng
        self.ov = []


class Op:
    __slots__ = ("eng", "fn", "deps", "is_dma", "signal", "sem", "semval")

    def __init__(self, eng, fn, is_dma):
        self.eng = eng
        self.fn = fn
        self.is_dma = is_dma
        self.deps = []
        self.signal = is_dma
        self.sem = None
        self.semval = 0


class Sched:
    def __init__(self, nc, n_dma_sems=6, same_engine_sync=True):
        self.nc = nc
        self.streams = {e: [] for e in ("pe", "act", "dve", "pool", "sp")}
        self.n_dma_sems = n_dma_sems
        self.dma_hist = {e: [] for e in DMAQ}
        self.same_engine_sync = same_engine_sync
        self.final_ops = []
        self.arena_bufs = []
        self.plan = False

    def arena_buf(self, name, lo, hi):
        b = Buf(name, (lo, hi))
        if self.plan:
            return b
        for o in self.arena_bufs:
            if o.rng[0] < hi and lo < o.rng[1]:
                o.ov.append(b)
                b.ov.append(o)
        self.arena_bufs.append(b)
        return b

    def op(self, eng, fn, reads=(), writes=(), dma=False, final=False):
        if self.plan:
            return None
        o = Op(eng, fn, dma)
        deps = []
        for b in reads:
            if b.last_w is not None:
                deps.append(b.last_w)
            if b.excl:
                deps.extend(b.rd.values())
            for x in b.ov:
                if x.last_w is not None:
                    deps.append(x.last_w)
        for b in writes:
            if b.last_w is not None:
                deps.append(b.last_w)
            deps.extend(b.rd.values())
            deps.extend(b.rd_dma)
            for x in b.ov:
                if x.last_w is not None:
                    deps.append(x.last_w)
                deps.extend(x.rd.values())
                deps.extend(x.rd_dma)
        if dma:
            h = self.dma_hist[eng]
            if len(h) >= self.n_dma_sems:
                deps.append(h[-self.n_dma_sems])
            h.append(o)
        seen = set()
        for d in deps:
            if d is o or id(d) in seen:
                continue
            seen.add(id(d))
            if (not d.is_dma) and d.eng == eng and not dma:
                if eng == "pe" or not self.same_engine_sync:
                    continue
            o.deps.append(d)
            d.signal = True
        for b in writes:
            b.last_w = o
            b.rd = {}
            b.rd_dma = []
        for b in reads:
            if b.last_w is not o:
                if dma:
                    b.rd_dma.append(o)
                else:
                    b.rd[eng] = o
        self.streams[eng].append(o)
        if final:
            self.final_ops.append(o)
        return o

    def emit(self):
        nc = self.nc
        with contextlib.ExitStack() as st:
            sems = {e: st.enter_context(nc.semaphore("s_" + e)) for e in COMPUTE}
            dsems = {e: [st.enter_context(nc.semaphore("d_%s%d" % (e, i))) for i in range(self.n_dma_sems)]
                     for e in DMAQ}
            for e in COMPUTE:
                cnt = 0
                for o in self.streams[e]:
                    if o.is_dma:
                        continue
                    if o.signal:
                        cnt += 1
                        o.sem = sems[e]
                        o.semval = cnt
            for e in DMAQ:
                cnts = [0] * self.n_dma_sems
                i = 0
                for o in self.streams[e]:
                    if not o.is_dma:
                        continue
                    k = i % self.n_dma_sems
                    cnts[k] += 16
                    o.sem = dsems[e][k]
                    o.semval = cnts[k]
                    i += 1
            block = st.enter_context(nc.Block())

            def run(ename, eng):
                waited = {}
                for o in self.streams[ename]:
                    for d in o.deps:
                        key = id(d.sem)
                        if waited.get(key, 0) >= d.semval:
                            continue
                        eng.wait_ge(d.sem, d.semval)
                        waited[key] = d.semval
                    ins = o.fn(eng)
                    if o.signal:
                        ins.then_inc(o.sem, 16 if o.is_dma else 1)
                if ename == "sp":
                    for o in self.final_ops:
                        key = id(o.sem)
                        if waited.get(key, 0) >= o.semval:
                            continue
                        eng.wait_ge(o.sem, o.semval)
                        waited[key] = o.semval

            @block.tensor
            def _(e):
                run("pe", e)

            @block.scalar
            def _(e):
                run("act", e)

            @block.vector
            def _(e):
                run("dve", e)

            @block.gpsimd
            def _(e):
                run("pool", e)

            @block.sync
            def _(e):
                run("sp", e)


class StopBuild(Exception):
    pass


STOP = None
DUMPS = []


def checkpoint(name):
    if STOP == name:
        raise StopBuild(name)


class Rot:
    def __init__(self, items):
        self.items = items
        self.i = 0

    def next(self):
        x = self.items[self.i % len(self.items)]
        self.i += 1
        return x


def rs_of(r):
    return min(max(r - 4, 0), ROWS - 8)


def build_program(depth=DEPTH, do_S=True, do_P=True, s_tiles=4):
    nc = bass.Bass("TRN2", target_bir_lowering=False)
    S = Sched(nc)

    def din(name, shape, dt=F32):
        return nc.dram_tensor(name, list(shape), dt, kind="ExternalInput").ap()

    def dout(name, shape):
        return nc.dram_tensor(name, list(shape), F32, kind="ExternalOutput").ap()

    xs_d = din("xs", [2048, D])
    xp_d = din("xp", [512, D])
    ck_d = din("ck", [DEPTH, NH, 256, 64])
    cv_d = din("cv", [DEPTH, NH, 256, 64])
    condT_d = din("condT", [128, 8, 2])
    bada_d = din("bada", [128, DEPTH, 48])
    vec8_d = din("vec8", [128, DEPTH, 5, 8])
    fing_d = din("fing", [128, 8])
    dww_d = din("dww", [128, DEPTH, 8, 31])
    scw_d = din("scw", [128, DEPTH, 8, 3])
    rpbR_d = din("rpbR", [DEPTH, NH, 128, 15])
    cmask_d = din("cmask", [128, 64])
    identf_d = din("identf", [128, 128])
    jrev_d = din("jrev", [128, 128])
    w_ada_d = din("w_ada", [DEPTH, D, 6 * D])
    w_in_d = din("w_in", [DEPTH, D, D_IN])
    sq_d = {n: din(n, [DEPTH, D, D]) for n in ("conv_pw_w", "sc_out_w", "na_out_w", "w_o")}
    wg_d = din("ffn_w_gate", [DEPTH, D, D_FF])
    wu_d = din("ffn_w_up", [DEPTH, D, D_FF])
    wd_d = din("ffn_w_down", [DEPTH, D_FF, D])
    ys_d = dout("ys", [2048, D])
    yp_d = dout("yp", [512, D])
    nk_d = dout("nk", [2, DEPTH, NH, 256, 64])
    nv_d = dout("nv", [2, DEPTH, NH, 256, 64])

    blocks = {}
    order_layer = []

    def add_block(bid, kc, nb, casts):
        blocks[bid] = dict(idx=len(blocks), kc=kc, nb=nb, casts=casts)

    for l in range(depth):
        wa = w_ada_d[l].rearrange("(kc p) n -> p kc n", p=128)
        wi = w_in_d[l].rearrange("(kc p) n -> p kc n", p=128)
        for b in range(12):
            add_block((l, "ada", b), 8, 512, [(0, 512, wa[:, :, 512 * b:512 * b + 512])])
        for b in range(22):
            add_block((l, "win", b), 8, 512, [(0, 512, wi[:, :, 512 * b:512 * b + 512])])
        for n, key in (("conv_pw_w", "pw"), ("sc_out_w", "sco"), ("na_out_w", "nao"), ("w_o", "wo")):
            ws_ = sq_d[n][l].rearrange("(kc p) n -> p kc n", p=128)
            for b in range(2):
                add_block((l, key, b), 8, 512, [(0, 512, ws_[:, :, 512 * b:512 * b + 512])])
        g_ = wg_d[l].rearrange("(kc p) n -> p kc n", p=128)
        u_ = wu_d[l].rearrange("(kc p) n -> p kc n", p=128)
        for b in range(11):
            add_block((l, "ffn", b), 8, 512, [(0, 256, g_[:, :, 256 * b:256 * b + 256]),
                                             (256, 256, u_[:, :, 256 * b:256 * b + 256])])
        d_ = wd_d[l].rearrange("(kc p) n -> p kc n", p=128)
        for b in range(8):
            add_block((l, "dn", b), 22, 128, [(0, 128, d_[:, :, 128 * b:128 * b + 128])])
    wsc_d = nc.dram_tensor("wsc", [len(blocks), 128, SLOT_E], BF16, kind="Internal").ap()
    Gs_d = nc.dram_tensor("Gs", [DEPTH, 128, 8 * 15 * 64], BF16, kind="Internal").ap()
    KTs_d = nc.dram_tensor("KTs", [DEPTH, 128, 8 * 256], BF16, kind="Internal").ap()
    bGs = [Buf("Gs%d" % l) for l in range(DEPTH)]
    bKTs = [Buf("KTs%d" % l) for l in range(DEPTH)]
    bscr = {bid: Buf("scr%d" % blk["idx"]) for bid, blk in blocks.items()}

    def tile_blocks(l):
        seq = []
        for hg in range(2):
            seq += [(l, "win", 4 + hg), (l, "win", 6 + hg), (l, "win", 8 + hg)]
        for h in range(2):
            seq += [(l, "nao", h), (l, "win", 18 + h)]
        for h in range(2):
            seq += [(l, "win", 0 + h), (l, "win", 2 + h)]
        for h in range(2):
            seq += [(l, "pw", h), (l, "win", 16 + h)]
        for h in range(2):
            seq += [(l, "win", 12 + h), (l, "win", 14 + h)]
        seq += [(l, "win", 10), (l, "win", 11)]
        for h in range(2):
            seq += [(l, "sco", h), (l, "win", 20 + h)]
        seq += [(l, "wo", 0), (l, "wo", 1)]
        seq += [(l, "ffn", b) for b in range(11)]
        seq += [(l, "dn", b) for b in range(8)]
        return seq

    def ada_blocks(l):
        return [(l, "ada", b) for b in range(12)]

    with contextlib.ExitStack() as st:
        def sb(name, shape, dt):
            return st.enter_context(nc.sbuf_tensor(name, list(shape), dt))

        xT = sb("xT", [128, 8, 2048], F32)
        uw = sb("uw", [128, 8, UW], BF16)
        ring = [sb("ring%d" % i, [128, SLOT_E], BF16) for i in range(NSLOT)]
        G = sb("G", [128, 8, 15, 64], BF16)
        ctxKT = sb("ctxKT", [128, 8, 256], BF16)
        ctxV = sb("ctxV", [128, 2, 1024], BF16)
        yat = sb("yat", [128, 8, 512], BF16)
        mixed = sb("mixed", [128, 8, 512], BF16)
        arena_t = sb("arena", [128, ARENA_BYTES], U8)
        tmpf_t = [sb("tmpf%d" % i, [128, 576], F32) for i in range(4)]
        tmpb_t = [sb("tmpb%d" % i, [128, 576], BF16) for i in range(4)]
        rsd_t = [sb("rsd%d" % i, [128, 512], F32) for i in range(2)]
        modt = sb("modt", [128, DEPTH, 48, 2], F32)
        A1t = sb("A1t", [128, DEPTH, 8, 2], F32)
        A2t = sb("A2t", [128, DEPTH, 8, 2], F32)
        condT = sb("condT_s", [128, 8, 2], F32)
        condS = sb("condS", [128, 8, 2], BF16)
        bada = sb("bada_s", [128, DEPTH, 48], F32)
        vec8 = sb("vec8_s", [128, DEPTH, 5, 8], F32)
        fing = sb("fing_s", [128, 8], F32)
        dww = sb("dww_s", [128, DEPTH, 8, 31], F32)
        scw = sb("scw_s", [128, DEPTH, 8, 3], F32)
        cmask = sb("cmask_s", [128, 64], F32)
        identf = sb("identf_s", [128, 128], F32)
        identb = sb("identb_s", [128, 128], BF16)
        jrevf = sb("jrevf_s", [128, 128], F32)
        jrevb = sb("jrevb_s", [128, 128], BF16)
        onesb = sb("onesb", [128, 128], BF16)
        onesbd = sb("onesbd", [128, 128], BF16)
        ps = [st.enter_context(nc.psum_tensor("ps%d" % i, [128, 512], F32)) for i in range(8)]

        bps = [Buf("ps%d" % i, excl=True) for i in range(8)]
        bx = [[Buf("x%d_%d" % (t, j)) for j in range(8)] for t in range(4)]
        bu = [Buf("u%d" % k) for k in range(8)]
        bslot = [Buf("slot%d" % i) for i in range(NSLOT)]
        bG, bctxK, bctxV = Buf("G"), Buf("ctxK"), Buf("ctxV")
        byat = [Buf("yat%d" % j) for j in range(8)]
        bmix = [Buf("mix%d" % j) for j in range(8)]
        bconst = Buf("const")
        bmod = Buf("mod")
        tmpf = Rot([(t[:], Buf("tmpf%d" % i)) for i, t in enumerate(tmpf_t)])
        tmpb = Rot([(t[:], Buf("tmpb%d" % i)) for i, t in enumerate(tmpb_t)])
        rsd = Rot([(t[:], Buf("rsd%d" % i)) for i, t in enumerate(rsd_t)])
        bank_main = Rot([0, 1])
        bank_aux = Rot([2, 3])
        bank_s = Rot([4, 5])
        BN, BD = 6, 7

        def MM(out, lhsT, rhs, start, stop, reads, writes, skip=False):
            S.op("pe", lambda e: e.matmul(out, lhsT=lhsT, rhs=rhs, start=start, stop=stop, skip_group_check=skip),
                 reads=reads, writes=writes)

        def TR(out, in_, ident, reads, writes):
            S.op("pe", lambda e: e.transpose(out=out, in_=in_, identity=ident), reads=reads, writes=writes)

        def ACT(out, in_, func, reads, writes, bias=None, scale=None):
            kw = {}
            if bias is not None:
                kw["bias"] = bias
            if scale is not None:
                kw["scale"] = scale
            S.op("act", lambda e: e.activation(out=out, in_=in_, func=func, **kw), reads=reads, writes=writes)

        def TT(out, in0, in1, op, reads, writes, eng="dve"):
            S.op(eng, lambda e: e.tensor_tensor(out=out, in0=in0, in1=in1, op=op), reads=reads, writes=writes)

        def TS(out, in0, s1, s2, op0, op1, reads, writes, eng="dve"):
            if op1 is None:
                S.op(eng, lambda e: e.tensor_scalar(out=out, in0=in0, scalar1=s1, scalar2=None, op0=op0),
                     reads=reads, writes=writes)
            else:
                S.op(eng, lambda e: e.tensor_scalar(out=out, in0=in0, scalar1=s1, scalar2=s2, op0=op0, op1=op1),
                     reads=reads, writes=writes)

        def STT(out, in0, scalar, in1, op0, op1, reads, writes):
            S.op("dve", lambda e: e.scalar_tensor_tensor(out=out, in0=in0, scalar=scalar, in1=in1, op0=op0, op1=op1),
                 reads=reads, writes=writes)

        def CPY(eng, out, in_, reads, writes):
            if eng == "act":
                ACT(out, in_, AF.Identity, reads, writes)
            else:
                S.op(eng, lambda e: e.tensor_copy(out=out, in_=in_), reads=reads, writes=writes)

        def MEMSET(eng, ap, val, writes):
            S.op(eng, lambda e: e.memset(ap, val), writes=writes)

        def DMA(eng, out, in_, reads, writes, final=False):
            return S.op(eng, lambda e: e.dma_start(out=out, in_=in_), reads=reads, writes=writes, dma=True, final=final)

        class Arena:
            def __init__(self):
                self.off = 0

            def reset(self):
                self.off = 0

            def alloc(self, name, n, dt):
                esz = 4 if dt == F32 else 2
                nb = (n * esz + 31) // 32 * 32
                lo, hi = self.off, self.off + nb
                assert hi <= ARENA_BYTES, (name, hi)
                self.off = hi
                ap = arena_t[:, lo:lo + n * esz].bitcast(dt)
                return ap, S.arena_buf(name, lo, hi)

        arena = Arena()

        class WS:
            def __init__(self):
                self.seq = []
                self.pos = 0
                self.issued = 0
                self.cast_done = set()
                self.cast_queue = []

            def cast(self, bid):
                if bid in self.cast_done:
                    return
                self.cast_done.add(bid)
                blk = blocks[bid]
                dst = wsc_d[blk["idx"]][:, 0:blk["kc"] * blk["nb"]].rearrange("p (k n) -> p k n", n=blk["nb"])
                for (c0, ncol, src) in blk["casts"]:
                    DMA("pool", dst[:, :, c0:c0 + ncol], src, [], [bscr[bid]])

            def cast_some(self, n):
                if S.plan:
                    return
                while n > 0 and self.cast_queue:
                    self.cast(self.cast_queue.pop(0))
                    n -= 1

            def _issue(self, n):
                while self.issued <= n and self.issued < len(self.seq):
                    bid = self.seq[self.issued]
                    self.cast(bid)
                    blk = blocks[bid]
                    sl = self.issued % NSLOT
                    ne = blk["kc"] * blk["nb"]
                    DMA("sp", ring[sl][:, 0:ne], wsc_d[blk["idx"]][:, 0:ne], [bscr[bid]], [bslot[sl]])
                    self.issued += 1

            def get(self, bid):
                if S.plan:
                    self.seq.append(bid)
                    blk = blocks[bid]
                    return ring[0][:, 0:blk["kc"] * blk["nb"]].rearrange("p (k n) -> p k n", n=blk["nb"]), bslot[0]
                assert self.seq[self.pos] == bid, (self.pos, self.seq[self.pos], bid)
                self._issue(self.pos + NSLOT - 2)
                sl = self.pos % NSLOT
                self.pos += 1
                blk = blocks[bid]
                view = ring[sl][:, 0:blk["kc"] * blk["nb"]].rearrange("p (k n) -> p k n", n=blk["nb"])
                return view, bslot[sl]

        ws = WS()
        wsh = [ws]

        def emit_consts():
            for dst, src in ((condT, condT_d), (bada, bada_d), (vec8, vec8_d), (fing, fing_d), (dww, dww_d),
                             (scw, scw_d), (cmask, cmask_d), (identf, identf_d), (jrevf, jrev_d)):
                DMA("sp", dst[:], src, [], [bconst])
            CPY("dve", identb[:], identf[:], [bconst], [bconst])
            CPY("dve", jrevb[:], jrevf[:], [bconst], [bconst])
            MEMSET("dve", onesb[:], 1.0, [bconst])
            MEMSET("dve", onesbd[:], 1.0, [bconst])
            MEMSET("dve", onesbd[0:64, 64:128], 0.0, [bconst])
            MEMSET("dve", onesbd[64:128, 0:64], 0.0, [bconst])
            for k in range(8):
                MEMSET("dve", uw[:, k, :], 0.0, [bu[k]])
            ACT(condS[:], condT[:], AF.Silu, [bconst], [bconst])

        def proj(pb, ncols, wview, wcols, rhs_fn, nk, wbuf, rhs_bufs_fn, pcol0=0):
            for k in range(nk):
                MM(ps[pb][:, pcol0:pcol0 + ncols], wview[:, k, wcols], rhs_fn(k), k == 0, k == nk - 1,
                   [wbuf] + rhs_bufs_fn(k), [bps[pb]])

        def adaln(l):
            pb = bank_main.next()
            for b in range(12):
                W, wb = ws.get((l, "ada", b))
                for jj in range(4):
                    ch = 4 * b + jj
                    for k in range(8):
                        MM(ps[pb][:, 2 * ch:2 * ch + 2], W[:, k, jj * 128:(jj + 1) * 128], condS[:, k, :], k == 0, k == 7,
                           [wb, bconst], [bps[pb]])
            pv = ps[pb][:, 0:96].rearrange("p (a b) -> p a b", b=2)
            for c in range(2):
                TT(modt[:, l, :, c], pv[:, :, c], bada[:, l, :], ALU.add, [bps[pb], bconst], [bmod])
            for c in range(2):
                STT(A1t[:, l, :, c], modt[:, l, 8:16, c], 1.0, vec8[:, l, 0, :], ALU.add, ALU.mult, [bmod, bconst], [bmod])
                STT(A2t[:, l, :, c], modt[:, l, 32:40, c], 1.0, vec8[:, l, 1, :], ALU.add, ALU.mult, [bmod, bconst], [bmod])

        def rms_modulate(n, xsrc, xbufs, dst, dbufs, scale_ap, shift_ap):
            pb = bank_s.next()
            for j in range(8):
                sq, sqb = tmpb.next()
                ACT(sq[:, 0:n], xsrc(j), AF.Square, [xbufs(j)], [sqb])
                MM(ps[pb][:, 0:n], onesb[:], sq[:, 0:n], j == 0, j == 7, [bconst, sqb], [bps[pb]])
            ln_, lnb = tmpf.next()
            ACT(ln_[:, 0:n], ps[pb][:, 0:n], AF.Ln, [bps[pb]], [lnb], bias=EPS, scale=1.0 / D)
            rstd, rsb = rsd.next()
            ACT(rstd[:, 0:n], ln_[:, 0:n], AF.Exp, [lnb], [rsb], scale=-0.5)
            for j in range(8):
                t, tb = tmpf.next()
                TT(t[:, 0:n], xsrc(j), rstd[:, 0:n], ALU.mult, [xbufs(j), rsb], [tb])
                sh = shift_ap(j)
                ACT(dst(j), t[:, 0:n], AF.Identity, [tb, bmod, bconst], [dbufs(j)], bias=(sh if sh is not None else 0.0),
                    scale=scale_ap(j))

        def mix_out(l, cnd, ysrc, ybufs, wname, gate_blk, first):
            for half in range(2):
                W, wb = ws.get((l, wname, half))
                Wg, wgb = ws.get((l, "win", gate_blk + half))
                for jj in range(4):
                    j = 4 * half + jj
                    pb = bank_main.next()
                    proj(pb, 512, W, slice(jj * 128, (jj + 1) * 128), ysrc, 8, wb, lambda k: [ybufs(k)])
                    gb = bank_aux.next()
                    proj(gb, 512, Wg, slice(jj * 128, (jj + 1) * 128), lambda k: uw[:, k, LEFT:LEFT + 512], 8, wgb,
                         lambda k: [bu[k]])
                    g, gbuf = tmpf.next()
                    ACT(g[:, 0:512], ps[gb][:, 0:512], AF.Sigmoid, [bps[gb]], [gbuf])
                    if first:
                        TT(mixed[:, j, :], ps[pb][:, 0:512], g[:, 0:512], ALU.mult, [bps[pb], gbuf], [bmix[j]])
                    else:
                        t2, t2b = tmpf.next()
                        TT(t2[:, 0:512], ps[pb][:, 0:512], g[:, 0:512], ALU.mult, [bps[pb], gbuf], [t2b])
                        TT(mixed[:, j, :], mixed[:, j, :], t2[:, 0:512], ALU.add, [bmix[j], t2b], [bmix[j]])

        def load_ctxv(l):
            for c in range(2):
                DMA("pool", ctxV[:, c, :].rearrange("p (h d) -> p h d", d=64),
                    cv_d[l, :, c * 128:(c + 1) * 128, :].rearrange("h s d -> s h d"), [], [bctxV])

        def tables_pre():
            for l in reversed(range(depth)):
                arena.reset()
                hst = [arena.alloc("hstf%d" % i, 960, F32) for i in range(2)]
                kstf = [arena.alloc("kstf%d" % c, 16 * 64, F32) for c in range(2)]
                for hp in range(8):
                    h_, hb_ = hst[hp % 2]
                    for s in range(2):
                        src = bass.AP(rpbR_d.tensor, ((l * NH) + 2 * hp + s) * 15 * 128, [[15, 64], [1, 960]])
                        DMA("act", h_[64 * s:64 * s + 64, :], src, [], [hb_])
                    cm = cmask[:]
                    for w0 in (0, 32):
                        pb = bank_aux.next()
                        MM(ps[pb][:, 0:480], jrevf[:], h_[:, w0 * 15:(w0 + 32) * 15], True, True, [bconst, hb_], [bps[pb]])
                        cmb = bass.AP(cm.tensor, cm.offset + w0, [cm.ap[0], [0, 15], [1, 32]])
                        TT(G[:, hp, :, w0:w0 + 32], ps[pb][:, 0:480].rearrange("p (w e) -> p e w", e=15), cmb, ALU.add,
                           [bps[pb], bconst], [bG])
                for c in range(2):
                    DMA("act", kstf[c][0].rearrange("p (h d) -> p h d", d=64),
                        ck_d[l, :, c * 128:(c + 1) * 128, :].rearrange("h s d -> s h d"), [], [kstf[c][1]])
                for hp in range(8):
                    pb = bank_aux.next()
                    for c in range(2):
                        TR(ps[pb][:, c * 128:(c + 1) * 128], kstf[c][0][:, 2 * hp * 64:(2 * hp + 2) * 64], identf[:],
                           [kstf[c][1], bconst], [bps[pb]])
                    CPY("dve", ctxKT[:, hp, :], ps[pb][:, 0:256], [bps[pb]], [bctxK])
                if l > 0:
                    DMA("act", Gs_d[l], G[:].rearrange("p a b c -> p (a b c)"), [bG], [bGs[l]])
                    DMA("act", KTs_d[l], ctxKT[:].rearrange("p a b -> p (a b)"), [bctxK], [bKTs[l]])
            load_ctxv(0)

        def build_layer_tables(l):
            if l == 0:
                return
            DMA("sp", G[:].rearrange("p a b c -> p (a b c)"), Gs_d[l], [bGs[l]], [bG])
            DMA("sp", ctxKT[:].rearrange("p a b -> p (a b)"), KTs_d[l], [bKTs[l]], [bctxK])
            load_ctxv(l)

        def norm1(l, kind, ti):
            cnd = 0 if kind == "S" else 1
            own0 = 512 * ti
            A1 = lambda j: A1t[:, l, j, cnd:cnd + 1]
            mod = lambda idx, j: modt[:, l, idx * 8 + j, cnd:cnd + 1]
            if kind == "S" and ti > 0:
                for k in range(8):
                    CPY("act", uw[:, k, LEFT - 256:LEFT], uw[:, k, LEFT + 256:LEFT + 512], [bu[k]], [bu[k]])
            ranges = [(ti, own0, 512, LEFT)]
            if kind == "S" and ti < 3:
                ranges.append((ti + 1, own0 + 512, 256, LEFT + 512))
            for (tsrc, xc, n, uc) in ranges:
                rms_modulate(n, lambda j: xT[:, j, xc:xc + n], lambda j: bx[tsrc][j],
                             lambda j: uw[:, j, uc:uc + n], lambda j: bu[j], A1, lambda j: mod(0, j))

        def tile_prog(l, kind, ti, is_last, hoist):
            cnd = 0 if kind == "S" else 1
            own0 = 512 * ti
            xb_own = bx[ti]
            if kind == "S":
                r0 = 8 * ti
                kmin, kmax = rs_of(r0), rs_of(r0 + 7) + 7
                nrows = kmax - kmin + 1
                nseg, L = 1, 512
            else:
                nseg, L = 2, 256
            A1 = lambda j: A1t[:, l, j, cnd:cnd + 1]
            A2 = lambda j: A2t[:, l, j, cnd:cnd + 1]
            mod = lambda idx, j: modt[:, l, idx * 8 + j, cnd:cnd + 1]
            uown = lambda k: uw[:, k, LEFT:LEFT + 512]
            ubuf = lambda k: [bu[k]]

            if STOP == "norm1":
                DUMPS.append(("uw", uw[:, :, LEFT:LEFT + 512], bu))
                DUMPS.append(("modt", modt[:, 0], [bmod]))
                DUMPS.append(("A1t", A1t[:, 0], [bmod]))
                DUMPS.append(("xT", xT[:, :, 0:512], bx[0]))
            checkpoint("norm1")
            for hg in range(2):
                arena.reset()
                QT = [arena.alloc("QT%d" % jj, 512, BF16) for jj in range(4)]
                if kind == "S":
                    KTbd = [arena.alloc("KTbd%d" % jj, 15 * 128, BF16) for jj in range(4)]
                    Vbd = [arena.alloc("Vbd%d" % jj, 15 * 128, BF16) for jj in range(4)]
                    for jj in range(4):
                        for (t_, tb_) in (KTbd[jj], Vbd[jj]):
                            tv = t_.rearrange("p (q m) -> p q m", m=128)
                            MEMSET("dve", tv[0:64, :, 64:128], 0.0, [tb_])
                            MEMSET("dve", tv[64:128, :, 0:64], 0.0, [tb_])
                    KT = VA = None
                else:
                    KT = [arena.alloc("KT%d" % jj, UW, BF16) for jj in range(4)]
                    VA = [arena.alloc("VA%d" % m, 512, BF16) for m in range(4)]
                W, wb = ws.get((l, "win", 4 + hg))
                for jj in range(4):
                    pb = bank_main.next()
                    proj(pb, 512, W, slice(jj * 128, (jj + 1) * 128), uown, 8, wb, ubuf)
                    CPY("act", QT[jj][0], ps[pb][:, 0:512], [bps[pb]], [QT[jj][1]])
                checkpoint("q")
                W, wb = ws.get((l, "win", 6 + hg))
                if kind == "S":
                    c0, c1 = LEFT + (kmin - r0) * 64, LEFT + (kmax + 1 - r0) * 64
                else:
                    c0, c1 = LEFT, LEFT + 512
                pieces = []
                c = c0
                while c < c1:
                    n = min(512, c1 - c)
                    pieces.append((c, n))
                    c += n
                for jj in range(4):
                    for (c, n) in pieces:
                        pb = bank_main.next()
                        proj(pb, n, W, slice(jj * 128, (jj + 1) * 128), lambda k, c=c, n=n: uw[:, k, c:c + n], 8, wb, ubuf)
                        if kind == "S":
                            q0 = (c - c0) // 64
                            nrp = n // 64
                            kv = KTbd[jj][0].rearrange("p (q m) -> p q m", m=128)
                            CPY("dve", kv[0:64, q0:q0 + nrp, 0:64], ps[pb][0:64, 0:n].rearrange("p (q m) -> p q m", m=64),
                                [bps[pb]], [KTbd[jj][1]])
                            CPY("act", kv[64:128, q0:q0 + nrp, 64:128], ps[pb][64:128, 0:n].rearrange("p (q m) -> p q m", m=64),
                                [bps[pb]], [KTbd[jj][1]])
                        else:
                            CPY("dve", KT[jj][0][:, c:c + n], ps[pb][:, 0:n], [bps[pb]], [KT[jj][1]])
                checkpoint("k")
                if kind == "P":
                    for tc in range(4):
                        pb = bank_aux.next()
                        for k in range(8):
                            MM(ps[pb][:, 0:512], uw[:, k, LEFT + 128 * tc:LEFT + 128 * tc + 128], W[:, k, :], k == 0, k == 7,
                               [wb, bu[k]], [bps[pb]])
                        stg, stb = tmpf.next()
                        CPY("act", stg[:, 0:512], ps[pb][:, 0:512], [bps[pb]], [stb])
                        dst = nk_d[tc // 2, l, 8 * hg:8 * hg + 8, (tc % 2) * 128:(tc % 2) * 128 + 128, :].rearrange("h s d -> s h d")
                        DMA("pool", dst, stg[:, 0:512].rearrange("p (h d) -> p h d", d=64), [stb], [], final=True)
                checkpoint("knk")
                W, wb = ws.get((l, "win", 8 + hg))
                if kind == "S":
                    nA = (nrows + 1) // 2
                    vbufs = [Vbd[jj][1] for jj in range(4)]
                    for m in range(nA):
                        col = LEFT + (kmin + 2 * m - r0) * 64
                        pb = bank_main.next()
                        for k in range(8):
                            MM(ps[pb][:, 0:512], uw[:, k, col:col + 128], W[:, k, :], k == 0, k == 7, [wb, bu[k]], [bps[pb]])
                        pv4 = ps[pb][:, 0:512].rearrange("p (j m) -> p j m", m=128)
                        for par in range(2):
                            q = 2 * m + par
                            if q >= nrows:
                                continue
                            pin = slice(64 * par, 64 * par + 64)
                            for s_, eng in ((0, "dve"), (1, "act")):
                                outs = []
                                for jj in range(4):
                                    vv = Vbd[jj][0].rearrange("p (q m) -> p q m", m=128)
                                    outs.append(vv[64 * s_:64 * s_ + 64, q, 64 * s_:64 * s_ + 64])
                                o0 = outs[0]
                                stride = 15 * 128 * 2 // 2
                                out4 = bass.AP(o0.tensor, o0.offset, [o0.ap[0], [Vbd[1][0].offset - Vbd[0][0].offset, 4], [1, 64]])
                                CPY(eng, out4, pv4[pin, :, 64 * s_:64 * s_ + 64], [bps[pb]], vbufs)
                else:
                    for tc in range(4):
                        col = LEFT + 128 * tc
                        pb = bank_main.next()
                        for k in range(8):
                            MM(ps[pb][:, 0:512], uw[:, k, col:col + 128], W[:, k, :], k == 0, k == 7, [wb, bu[k]], [bps[pb]])
                        stg, stb = tmpf.next()
                        CPY("act", stg[:, 0:512], ps[pb][:, 0:512], [bps[pb]], [stb])
                        CPY("dve", VA[tc][0], stg[:, 0:512], [stb], [VA[tc][1]])
                        dst = nv_d[tc // 2, l, 8 * hg:8 * hg + 8, (tc % 2) * 128:(tc % 2) * 128 + 128, :].rearrange("h s d -> s h d")
                        DMA("pool", dst, stg[:, 0:512].rearrange("p (h d) -> p h d", d=64), [stb], [], final=True)

                checkpoint("qkv")
                jobs = []
                for jj in range(4):
                    hp = 4 * hg + jj
                    kt, ktb = KT[jj] if kind == "P" else (None, None)
                    qt, qtb = QT[jj]
                    first = len(jobs)
                    if kind == "S":
                        for q in range(nrows):
                            kr = kmin + q
                            rr = [r for r in range(r0, r0 + 8) if rs_of(r) <= kr <= rs_of(r) + 7]
                            ra, rb = rr[0], rr[-1]
                            nr = rb - ra + 1
                            nq = 64 * nr
                            qc = (ra - r0) * 64
                            kcol = LEFT + (kr - r0) * 64
                            e0 = ra - kr + 7
                            jobs.append(dict(kind="loc", hp=hp, jj=jj, nq=nq, qc=qc, q=q, e0=e0, nr=nr, qt=qt, qtb=qtb))
                        for s_ in range(2):
                            for c in range(2):
                                jobs.append(dict(kind="den", hp=hp, jj=jj, s=s_, nq=512, qc=0, qt=qt, qtb=qtb,
                                                 klhs=ctxKT[64 * s_:64 * s_ + 64, hp, c * 128:(c + 1) * 128], kbuf=bctxK,
                                                 vlhs=ctxV[:, c, (2 * hp + s_) * 64:(2 * hp + s_) * 64 + 64], vbuf=bctxV))
                    else:
                        for sq in range(2):
                            for s_ in range(2):
                                for c in range(2):
                                    kc0 = LEFT + sq * 256 + c * 128
                                    jobs.append(dict(kind="den", hp=hp, jj=jj, s=s_, nq=256, qc=sq * 256, qt=qt, qtb=qtb,
                                                     klhs=kt[64 * s_:64 * s_ + 64, kc0:kc0 + 128], kbuf=ktb,
                                                     vlhs=VA[2 * sq + c][0][:, jj * 128 + 64 * s_:jj * 128 + 64 * s_ + 64],
                                                     vbuf=VA[2 * sq + c][1]))
                    jobs[first]["first"] = True
                    jobs[-1]["last"] = True
                s_banks = Rot([4, 5, 0, 1])
                nd_pairs = Rot([(6, 7), (2, 3)])
                cur_nd = {}

                def emitS(jb):
                    if jb.get("first"):
                        bn, bd = nd_pairs.next()
                        cur_nd[jb["hp"]] = (bn, bd)
                        MEMSET("dve", ps[bn][:], 0.0, [bps[bn]])
                        MEMSET("dve", ps[bd][:], 0.0, [bps[bd]])
                    sbk = s_banks.next()
                    jb["sbk"] = sbk
                    nq, qc, qt, qtb = jb["nq"], jb["qc"], jb["qt"], jb["qtb"]
                    if jb["kind"] == "loc":
                        kb_, kbb_ = KTbd[jb["jj"]]
                        MM(ps[sbk][:, 0:nq], kb_[:, jb["q"] * 128:(jb["q"] + 1) * 128], qt[:, qc:qc + nq], True, True,
                           [kbb_, qtb], [bps[sbk]])
                    else:
                        s_ = jb["s"]
                        MM(ps[sbk][:, 0:nq], jb["klhs"], qt[64 * s_:64 * s_ + 64, qc:qc + nq], True, True, [jb["kbuf"], qtb], [bps[sbk]])

                def emitE(jb):
                    sbk, nq = jb["sbk"], jb["nq"]
                    p, pbuf = tmpb.next()
                    if jb["kind"] == "loc":
                        t, tb = tmpf.next()
                        STT(t[:, 0:nq], ps[sbk][:, 0:nq], ATT_SCALE,
                            G[:, jb["hp"], jb["e0"]:jb["e0"] + jb["nr"], :].rearrange("p a b -> p (a b)"),
                            ALU.mult, ALU.add, [bps[sbk], bG], [tb])
                        ACT(p[:, 0:nq], t[:, 0:nq], AF.Exp, [tb], [pbuf])
                    else:
                        ACT(p[:, 0:nq], ps[sbk][:, 0:nq], AF.Exp, [bps[sbk]], [pbuf], scale=ATT_SCALE)
                    jb["p"] = (p, pbuf)

                def emitPV(jb):
                    bn, bd = cur_nd[jb["hp"]]
                    nq, qc, jj = jb["nq"], jb["qc"], jb["jj"]
                    p, pbuf = jb["p"]
                    if jb["kind"] == "loc":
                        vb_, vbb_ = Vbd[jj]
                        MM(ps[bn][:, qc:qc + nq], vb_[:, jb["q"] * 128:(jb["q"] + 1) * 128], p[:, 0:nq], False, True,
                           [vbb_, pbuf], [bps[bn]], skip=True)
                        MM(ps[bd][:, qc:qc + nq], onesbd[:], p[:, 0:nq], False, True, [bconst, pbuf], [bps[bd]], skip=True)
                    else:
                        s_ = jb["s"]
                        MM(ps[bn][64 * s_:64 * s_ + 64, qc:qc + nq], jb["vlhs"], p[:, 0:nq], False, True, [jb["vbuf"], pbuf],
                           [bps[bn]], skip=True)
                        MM(ps[bd][64 * s_:64 * s_ + 64, qc:qc + nq], onesb[:, 0:64], p[:, 0:nq], False, True, [bconst, pbuf],
                           [bps[bd]], skip=True)
                    if jb.get("last"):
                        hp = jb["hp"]
                        lnd, lndb = tmpf.next()
                        ACT(lnd[:, 0:512], ps[bd][:, 0:512], AF.Ln, [bps[bd]], [lndb])
                        rec, recb = tmpf.next()
                        ACT(rec[:, 0:512], lnd[:, 0:512], AF.Exp, [lndb], [recb], scale=-1.0)
                        TT(yat[:, hp, :], ps[bn][:, 0:512], rec[:, 0:512], ALU.mult, [bps[bn], recb], [byat[hp]])

                LA = 3
                for i in range(min(LA, len(jobs))):
                    emitS(jobs[i])
                for i in range(len(jobs)):
                    emitE(jobs[i])
                    if i + LA < len(jobs):
                        emitS(jobs[i + LA])
                    emitPV(jobs[i])

            if STOP == "attn":
                DUMPS.append(("yat", yat[:], byat))
            checkpoint("attn")
            mix_out(l, cnd, lambda k: yat[:, k, :], lambda k: byat[k], "nao", 18, True)
            if STOP in ("nao", "conv", "sc"):
                DUMPS.append(("mixed", mixed[:], bmix))
            checkpoint("nao")

            arena.reset()
            SEG = L + 30
            agl = [arena.alloc("agl%d" % j, 576, BF16) for j in range(8)]
            hb = [(agl[j][0][:, 0:512], agl[j][1]) for j in range(8)]
            diag = [arena.alloc("diag%d" % i, 31 * 128, BF16) for i in range(2)]
            mu, mub = arena.alloc("mu", 512, F32)
            rstd, rstdb = arena.alloc("rstd", 512, F32)
            murs, mursb = arena.alloc("murs", 512, F32)
            sides = []
            if kind == "S":
                if ti > 0:
                    sides.append((LEFT - 15, 0))
                if ti < 3:
                    sides.append((LEFT + 512, 15 + 512))
            for half in range(2):
                Wa, wab = ws.get((l, "win", 0 + half))
                Wg, wgb = ws.get((l, "win", 2 + half))
                for jj in range(4):
                    j = 4 * half + jj
                    cs = slice(jj * 128, (jj + 1) * 128)
                    a_, ab_ = agl[j]
                    av = a_[:, 0:nseg * SEG].rearrange("p (s c) -> p s c", c=SEG)
                    pa = bank_main.next()
                    proj(pa, 512, Wa, cs, uown, 8, wab, ubuf)
                    pg = bank_aux.next()
                    proj(pg, 512, Wg, cs, uown, 8, wgb, ubuf)
                    sg, sgb = tmpf.next()
                    ACT(sg[:, 0:512], ps[pg][:, 0:512], AF.Sigmoid, [bps[pg]], [sgb])
                    MEMSET("dve", av[:, :, 0:15], 0.0, [ab_])
                    MEMSET("dve", av[:, :, 15 + L:30 + L], 0.0, [ab_])
                    TT(av[:, :, 15:15 + L], ps[pa][:, 0:512].rearrange("p (s c) -> p s c", c=L),
                       sg[:, 0:512].rearrange("p (s c) -> p s c", c=L), ALU.mult, [bps[pa], sgb], [ab_])
                    for (ucol, acol) in sides:
                        ph = bank_s.next()
                        proj(ph, 15, Wa, cs, lambda k, ucol=ucol: uw[:, k, ucol:ucol + 15], 8, wab, ubuf, pcol0=0)
                        proj(ph, 15, Wg, cs, lambda k, ucol=ucol: uw[:, k, ucol:ucol + 15], 8, wgb, ubuf, pcol0=32)
                        sh_, shb = tmpf.next()
                        ACT(sh_[:, 0:15], ps[ph][:, 32:47], AF.Sigmoid, [bps[ph]], [shb])
                        TT(a_[:, acol:acol + 15], ps[ph][:, 0:15], sh_[:, 0:15], ALU.mult, [bps[ph], shb], [ab_])
            idb_ = identb[:]
            for j in range(8):
                a_, ab_ = agl[j]
                av = a_[:, 0:nseg * SEG].rearrange("p (s c) -> p s c", c=SEG)
                dg, dgb = diag[j % 2]
                dgv = dg.rearrange("p (k m) -> p k m", m=128)
                wj = dww[:, l, j, :]
                TT(dgv, bass.AP(idb_.tensor, idb_.offset, [idb_.ap[0], [0, 31], [1, 128]]),
                   bass.AP(wj.tensor, wj.offset, [wj.ap[0], [1, 31], [0, 128]]), ALU.mult, [bconst], [dgb])
                pb = bank_main.next()
                for k in range(31):
                    MM(ps[pb][:, 0:512].rearrange("p (s c) -> p s c", c=L), dgv[:, k, :], av[:, :, k:k + L], k == 0, k == 30,
                       [dgb, ab_], [bps[pb]])
                ACT(hb[j][0], ps[pb][:, 0:512], AF.Identity, [bps[pb], bconst], [hb[j][1]], bias=vec8[:, l, 2, j:j + 1])
            p0, p1 = bank_s.next(), bank_s.next()
            for j in range(8):
                sq, sqb = tmpb.next()
                ACT(sq[:, 0:512], hb[j][0], AF.Square, [hb[j][1]], [sqb])
                MM(ps[p0][:, 0:512], onesb[:], hb[j][0], j == 0, j == 7, [bconst, hb[j][1]], [bps[p0]])
                MM(ps[p1][:, 0:512], onesb[:], sq[:, 0:512], j == 0, j == 7, [bconst, sqb], [bps[p1]])
            ACT(mu, ps[p0][:, 0:512], AF.Identity, [bps[p0]], [mub], scale=1.0 / D)
            msq, msqb = tmpf.next()
            TT(msq[:, 0:512], mu, mu, ALU.mult, [mub], [msqb])
            var, varb = tmpf.next()
            STT(var[:, 0:512], ps[p1][:, 0:512], 1.0 / D, msq[:, 0:512], ALU.mult, ALU.subtract, [bps[p1], msqb], [varb])
            lnv, lnvb = tmpf.next()
            ACT(lnv[:, 0:512], var[:, 0:512], AF.Ln, [varb], [lnvb], bias=EPS)
            ACT(rstd, lnv[:, 0:512], AF.Exp, [lnvb], [rstdb], scale=-0.5)
            TT(murs, mu, rstd, ALU.mult, [mub, rstdb], [mursb])
            for j in range(8):
                t, tb = tmpf.next()
                TT(t[:, 0:512], hb[j][0], rstd, ALU.mult, [hb[j][1], rstdb], [tb])
                t2, t2b = tmpf.next()
                TT(t2[:, 0:512], t[:, 0:512], murs, ALU.subtract, [tb, mursb], [t2b])
                ACT(hb[j][0], t2[:, 0:512], AF.Silu, [t2b, bconst], [hb[j][1]], bias=vec8[:, l, 4, j:j + 1],
                    scale=vec8[:, l, 3, j:j + 1])
            mix_out(l, cnd, lambda k: hb[k][0], lambda k: hb[k][1], "pw", 16, False)
            checkpoint("conv")

            arena.reset()
            SEG2 = L + 2
            cx = [arena.alloc("cx%d" % j, 520, BF16) for j in range(8)]
            scy = [arena.alloc("scy%d" % j, 512, BF16) for j in range(8)]
            sides2 = []
            if kind == "S":
                if ti > 0:
                    sides2.append((LEFT - 1, 0))
                if ti < 3:
                    sides2.append((LEFT + 512, 1 + 512))
            for half in range(2):
                Wc, wcb = ws.get((l, "win", 12 + half))
                Wx, wxb = ws.get((l, "win", 14 + half))
                for jj in range(4):
                    j = 4 * half + jj
                    cs = slice(jj * 128, (jj + 1) * 128)
                    c_, cb_ = cx[j]
                    cvw = c_[:, 0:nseg * SEG2].rearrange("p (s c) -> p s c", c=SEG2)
                    pc_ = bank_main.next()
                    proj(pc_, 512, Wc, cs, uown, 8, wcb, ubuf)
                    px = bank_aux.next()
                    proj(px, 512, Wx, cs, uown, 8, wxb, ubuf)
                    t, tb = tmpf.next()
                    CPY("act", t[:, 0:512], ps[pc_][:, 0:512], [bps[pc_]], [tb])
                    MEMSET("dve", cvw[:, :, 0:1], 0.0, [cb_])
                    MEMSET("dve", cvw[:, :, 1 + L:2 + L], 0.0, [cb_])
                    TT(cvw[:, :, 1:1 + L], t[:, 0:512].rearrange("p (s c) -> p s c", c=L),
                       ps[px][:, 0:512].rearrange("p (s c) -> p s c", c=L), ALU.mult, [tb, bps[px]], [cb_])
                    for (ucol, ccol) in sides2:
                        ph = bank_s.next()
                        proj(ph, 1, Wc, cs, lambda k, ucol=ucol: uw[:, k, ucol:ucol + 1], 8, wcb, ubuf, pcol0=0)
                        proj(ph, 1, Wx, cs, lambda k, ucol=ucol: uw[:, k, ucol:ucol + 1], 8, wxb, ubuf, pcol0=8)
                        th, thb = tmpf.next()
                        CPY("act", th[:, 0:1], ps[ph][:, 0:1], [bps[ph]], [thb])
                        TT(c_[:, ccol:ccol + 1], th[:, 0:1], ps[ph][:, 8:9], ALU.mult, [thb, bps[ph]], [cb_])
            for half in range(2):
                Wb, wbb = ws.get((l, "win", 10 + half))
                for jj in range(4):
                    j = 4 * half + jj
                    c_, cb_ = cx[j]
                    cvw = c_[:, 0:nseg * SEG2].rearrange("p (s c) -> p s c", c=SEG2)
                    c3, c3b = tmpf.next()
                    c3v = c3[:, 0:512].rearrange("p (s c) -> p s c", c=L)
                    TS(c3v, cvw[:, :, 0:L], scw[:, l, j, 0:1], None, ALU.mult, None, [cb_, bconst], [c3b])
                    for k in (1, 2):
                        STT(c3v, cvw[:, :, k:k + L], scw[:, l, j, k:k + 1], c3v, ALU.mult, ALU.add, [cb_, bconst, c3b], [c3b])
                    pb = bank_main.next()
                    proj(pb, 512, Wb, slice(jj * 128, (jj + 1) * 128), uown, 8, wbb, ubuf)
                    TT(scy[j][0], ps[pb][:, 0:512], c3[:, 0:512], ALU.mult, [bps[pb], c3b], [scy[j][1]])
            mix_out(l, cnd, lambda k: scy[k][0], lambda k: scy[k][1], "sco", 20, False)
            checkpoint("sc")

            hoist()
            for half in range(2):
                W, wb = ws.get((l, "wo", half))
                for jj in range(4):
                    j = 4 * half + jj
                    pb = bank_main.next()
                    proj(pb, 512, W, slice(jj * 128, (jj + 1) * 128), lambda k: mixed[:, k, :], 8, wb, lambda k: [bmix[k]])
                    xo = xT[:, j, own0:own0 + 512]
                    STT(xo, ps[pb][:, 0:512], mod(2, j), xo, ALU.mult, ALU.add, [bps[pb], bmod, xb_own[j]], [xb_own[j]])

            if STOP in ("wo", "ffn"):
                DUMPS.append(("xT", xT[:, :, own0:own0 + 512], xb_own))
            checkpoint("wo")
            arena.reset()
            u2 = [arena.alloc("u2_%d" % k, 512, BF16) for k in range(8)]
            hm = [arena.alloc("hm%d" % c, 512, BF16) for c in range(NFF)]
            rms_modulate(512, lambda j: xT[:, j, own0:own0 + 512], lambda j: xb_own[j],
                         lambda j: u2[j][0], lambda j: u2[j][1], A2, lambda j: mod(3, j))
            for b in range(11):
                W, wb = ws.get((l, "ffn", b))
                for c2 in range(2):
                    hc = 2 * b + c2
                    pg = bank_main.next()
                    proj(pg, 512, W, slice(c2 * 128, (c2 + 1) * 128), lambda k: u2[k][0], 8, wb, lambda k: [u2[k][1]])
                    pu = bank_aux.next()
                    proj(pu, 512, W, slice(256 + c2 * 128, 256 + (c2 + 1) * 128), lambda k: u2[k][0], 8, wb, lambda k: [u2[k][1]])
                    t, tb = tmpf.next()
                    ACT(t[:, 0:512], ps[pg][:, 0:512], AF.Silu, [bps[pg]], [tb])
                    TT(hm[hc][0], t[:, 0:512], ps[pu][:, 0:512], ALU.mult, [tb, bps[pu]], [hm[hc][1]])
            for j in range(8):
                W, wb = ws.get((l, "dn", j))
                pb = bank_main.next()
                proj(pb, 512, W, slice(0, 128), lambda k: hm[k][0], NFF, wb, lambda k: [hm[k][1]])
                xo = xT[:, j, own0:own0 + 512]
                STT(xo, ps[pb][:, 0:512], mod(5, j), xo, ALU.mult, ALU.add, [bps[pb], bmod, xb_own[j]], [xb_own[j]])

            checkpoint("ffn")
            if is_last:
                arena.reset()
                yT = [arena.alloc("yT%d" % j, 512, F32) for j in range(8)]
                stg = [arena.alloc("stg%d" % i, 1024, F32) for i in range(2)]
                rms_modulate(512, lambda j: xT[:, j, own0:own0 + 512], lambda j: xb_own[j],
                             lambda j: yT[j][0], lambda j: yT[j][1], lambda j: fing[:, j:j + 1], lambda j: None)
                ydst = ys_d if kind == "S" else yp_d
                for tc in range(4):
                    sg_, sgb_ = stg[tc % 2]
                    for half in range(2):
                        pb = bank_main.next()
                        for q in range(4):
                            j = 4 * half + q
                            TR(ps[pb][:, q * 128:(q + 1) * 128], yT[j][0][:, tc * 128:(tc + 1) * 128], identf[:],
                               [yT[j][1], bconst], [bps[pb]])
                        CPY("act" if half == 0 else "dve", sg_[:, half * 512:(half + 1) * 512], ps[pb][:, 0:512], [bps[pb]], [sgb_])
                    DMA("pool", ydst[own0 + tc * 128:own0 + tc * 128 + 128, :], sg_, [sgb_], [], final=True)

        def load_x(src_d, ntc):
            arena.reset()
            stg = [arena.alloc("xin%d" % i, 1024, F32) for i in range(2)]
            for tc in range(ntc):
                sg_, sgb_ = stg[tc % 2]
                DMA("sp", sg_, src_d[tc * 128:(tc + 1) * 128, :], [], [sgb_])
                t = tc // 4
                for half in range(2):
                    pb = bank_main.next()
                    for q in range(4):
                        j = 4 * half + q
                        TR(ps[pb][:, q * 128:(q + 1) * 128], sg_[:, j * 128:(j + 1) * 128], identf[:], [sgb_, bconst], [bps[pb]])
                    CPY("act" if half == 0 else "dve", xT[:, 4 * half:4 * half + 4, tc * 128:(tc + 1) * 128],
                        ps[pb][:, 0:512].rearrange("p (a b) -> p a b", b=128), [bps[pb]], [bx[t][4 * half + q] for q in range(4)])

        per_tile_casts = (12 + 61 + s_tiles - 1) // s_tiles

        def emit_all():
            emit_consts()
            checkpoint("consts")
            if do_S:
                load_x(xs_d, min(16, 4 * (s_tiles + 1)))
                checkpoint("load")
                tables_pre()
                ws.cast_some(12 + 61)
                adaln(0)
                checkpoint("ada")
                build_layer_tables(0)
                checkpoint("tables")
                norm1(0, "S", 0)
                for l in range(depth):
                    for ti in range(s_tiles):
                        def hoist(l=l, ti=ti):
                            if ti + 1 < s_tiles:
                                norm1(l, "S", ti + 1)
                            elif l + 1 < depth:
                                adaln(l + 1)
                                build_layer_tables(l + 1)
                                norm1(l + 1, "S", 0)
                            elif do_P:
                                load_x(xp_d, 4)
                                norm1(0, "P", 0)
                        ws.cast_some(per_tile_casts)
                        tile_prog(l, "S", ti, l == depth - 1, hoist)
            if do_P:
                if not do_S:
                    ws.cast_some(12 + 61)
                    load_x(xp_d, 4)
                    checkpoint("load")
                    for l in range(depth):
                        adaln(l)
                    checkpoint("ada")
                    norm1(0, "P", 0)
                for l in range(depth):
                    if l > 0:
                        norm1(l, "P", 0)
                    tile_prog(l, "P", 0, l == depth - 1, lambda: None)

        S.plan = True
        try:
            emit_all()
        except StopBuild:
            pass
        seq = ws.seq
        S.plan = False
        S.arena_bufs = []
        ws = WS()
        ws.seq = seq
        seen = set()
        for bid in seq:
            if bid not in seen:
                seen.add(bid)
                ws.cast_queue.append(bid)
        for r_ in (tmpf, tmpb, rsd, bank_main, bank_aux, bank_s):
            r_.i = 0
        DUMPS.clear()
        try:
            emit_all()
            assert ws.pos == len(ws.seq), (ws.pos, len(ws.seq))
        except StopBuild as ex:
            print("STOPPED at", ex)
        for (name, ap, bufs) in DUMPS:
            dd = nc.dram_tensor("dbg_" + name, list(ap.shape), ap.dtype, kind="ExternalOutput").ap()
            DMA("sp", dd, ap, bufs, [], final=True)
        S.emit()
    return nc, S


def _fm(v):
    v = np.asarray(v, np.float32)
    lead = v.shape[:-1]
    r = v.reshape(lead + (8, 128))
    r = np.moveaxis(r, -1, 0)
    return np.ascontiguousarray(r)


_PROG = {}


def _host_consts():
    cols = np.arange(GW)
    cs = np.clip(cols - 8, 0, GW - 16)
    cm = np.full((128, 64), NEG, np.float32)
    for w in range(GW):
        for s in range(2):
            cm[64 * s + cs[w]:64 * s + cs[w] + 16, w] = 0.0
    jrev = np.zeros((128, 128), np.float32)
    for s in range(2):
        for c in range(64):
            jrev[64 * s + c, 64 * s + 63 - c] = 1.0
    return cm, np.eye(128, dtype=np.float32), jrev


def make_in_maps(inputs, depth=DEPTH):
    f32 = lambda a: np.ascontiguousarray(np.asarray(a, np.float32))
    x_prompt, x_sample = f32(inputs["x_prompt"]), f32(inputs["x_sample"])
    cache_k, cache_v = f32(inputs["cache_k"]), f32(inputs["cache_v"])
    c, c_ctx = f32(inputs["c"]), f32(inputs["c_ctx"])
    cm, ident, jrev = _host_consts()
    rpb = f32(inputs["na_rpb"])
    rpbR = np.zeros((DEPTH, NH, 128, 15), np.float32)
    rpbR[:, :, 48:79, :] = rpb[:, :, ::-1, ::-1].transpose(0, 1, 3, 2)
    bada = np.ascontiguousarray(np.asarray(inputs["b_ada"], np.float32).reshape(DEPTH, 48, 128).transpose(2, 0, 1))
    vec8 = np.stack([_fm(inputs[n]) for n in ("norm1_g", "norm2_g", "conv_dw_b", "conv_ln_g", "conv_ln_b")], axis=2)
    fing = _fm(inputs["final_g"])
    dww = np.ascontiguousarray(_fm(inputs["conv_dw_w"]).transpose(0, 1, 3, 2))
    scw = np.ascontiguousarray(_fm(inputs["sc_dw_w"]).transpose(0, 1, 3, 2))
    shared = dict(bada=bada, vec8=np.ascontiguousarray(vec8), fing=fing, dww=dww, scw=scw, rpbR=rpbR, cmask=cm, identf=ident,
                  jrev=jrev, w_ada=f32(inputs["w_ada"]), w_in=f32(inputs["w_in"]), conv_pw_w=f32(inputs["conv_pw_w"]),
                  sc_out_w=f32(inputs["sc_out_w"]), na_out_w=f32(inputs["na_out_w"]), w_o=f32(inputs["w_o"]),
                  ffn_w_gate=f32(inputs["ffn_w_gate"]), ffn_w_up=f32(inputs["ffn_w_up"]), ffn_w_down=f32(inputs["ffn_w_down"]))
    maps = []
    for core in range(8):
        cond = np.stack([c[core], c_ctx], axis=0)
        condT = np.ascontiguousarray(cond.reshape(2, 8, 128).transpose(2, 1, 0))
        m = dict(shared)
        m.update(xs=x_sample[core], xp=np.ascontiguousarray(x_prompt[2 * core:2 * core + 2].reshape(512, D)),
                 ck=cache_k[core], cv=cache_v[core], condT=condT)
        maps.append(m)
    return maps


def kernel(**inputs):
    if "nc" not in _PROG:
        _PROG["nc"], _ = build_program()
    nc = _PROG["nc"]
    maps = make_in_maps(inputs)
    res = run_bass_kernel_spmd(nc, maps, core_ids=list(range(8)))
    r = res.results
    y_sample = np.stack([r[i]["ys"] for i in range(8)], axis=0)
    y_prompt = np.concatenate([r[i]["yp"].reshape(2, 256, D) for i in range(8)], axis=0)
    new_k = np.concatenate([r[i]["nk"] for i in range(8)], axis=0)
    new_v = np.concatenate([r[i]["nv"] for i in range(8)], axis=0)
    return (y_prompt.astype(np.float32), y_sample.astype(np.float32), new_k.astype(np.float32), new_v.astype(np.float32))
```

```python
import contextlib
import numpy as np
import concourse.bass as bass
import concourse.mybir as mybir
from concourse.bass_utils import run_bass_kernel_spmd

F32 = mybir.dt.float32
BF16 = mybir.dt.bfloat16
U8 = mybir.dt.uint8
ALU = mybir.AluOpType
AF = mybir.ActivationFunctionType

D = 1024
NCH = 8
DEPTH = 4
NH = 16
GW = 64
ROWS = 32
D_FF = 2816
NFF = 22
D_IN = 11264
EPS = 1e-6
ATT_SCALE = 0.125
NEG = -30000.0
LEFT = 320
UW = 1088
NSLOT = 4
SLOT_E = 4096
ARENA_BYTES = 31 * 1024

COMPUTE = ("pe", "act", "dve", "pool")
DMAQ = ("act", "pool", "sp")


class Buf:
    __slots__ = ("name", "last_w", "rd", "rd_dma", "rng", "ov", "excl")

    def __init__(self, name, rng=None, excl=False):
        self.excl = excl
        self.name = name
        self.last_w = None
        self.rd = {}
        self.rd_dma = []
        self.rng = rng
        self.ov = []


class Op:
    __slots__ = ("eng", "fn", "deps", "is_dma", "signal", "sem", "semval")

    def __init__(self, eng, fn, is_dma):
        self.eng = eng
        self.fn = fn
        self.is_dma = is_dma
        self.deps = []
        self.signal = is_dma
        self.sem = None
        self.semval = 0


class Sched:
    def __init__(self, nc, n_dma_sems=6, same_engine_sync=True):
        self.nc = nc
        self.streams = {e: [] for e in ("pe", "act", "dve", "pool", "sp")}
        self.n_dma_sems = n_dma_sems
        self.dma_hist = {e: [] for e in DMAQ}
        self.same_engine_sync = same_engine_sync
        self.final_ops = []
        self.arena_bufs = []
        self.plan = False

    def arena_buf(self, name, lo, hi):
        b = Buf(name, (lo, hi))
        if self.plan:
            return b
        for o in self.arena_bufs:
            if o.rng[0] < hi and lo < o.rng[1]:
                o.ov.append(b)
                b.ov.append(o)
        self.arena_bufs.append(b)
        return b

    def op(self, eng, fn, reads=(), writes=(), dma=False, final=False):
        if self.plan:
            return None
        o = Op(eng, fn, dma)
        deps = []
        for b in reads:
            if b.last_w is not None:
                deps.append(b.last_w)
            if b.excl:
                deps.extend(b.rd.values())
            for x in b.ov:
                if x.last_w is not None:
                    deps.append(x.last_w)
        for b in writes:
            if b.last_w is not None:
                deps.append(b.last_w)
            deps.extend(b.rd.values())
            deps.extend(b.rd_dma)
            for x in b.ov:
                if x.last_w is not None:
                    deps.append(x.last_w)
                deps.extend(x.rd.values())
                deps.extend(x.rd_dma)
        if dma:
            h = self.dma_hist[eng]
            if len(h) >= self.n_dma_sems:
                deps.append(h[-self.n_dma_sems])
            h.append(o)
        seen = set()
        for d in deps:
            if d is o or id(d) in seen:
                continue
            seen.add(id(d))
            if (not d.is_dma) and d.eng == eng and not dma:
                if eng == "pe" or not self.same_engine_sync:
                    continue
            o.deps.append(d)
            d.signal = True
        for b in writes:
            b.last_w = o
            b.rd = {}
            b.rd_dma = []
        for b in reads:
            if b.last_w is not o:
                if dma:
                    b.rd_dma.append(o)
                else:
                    b.rd[eng] = o
        self.streams[eng].append(o)
        if final:
            self.final_ops.append(o)
        return o

    def emit(self):
        nc = self.nc
        with contextlib.ExitStack() as st:
            sems = {e: st.enter_context(nc.semaphore("s_" + e)) for e in COMPUTE}
            dsems = {e: [st.enter_context(nc.semaphore("d_%s%d" % (e, i))) for i in range(self.n_dma_sems)]
                     for e in DMAQ}
            for e in COMPUTE:
                cnt = 0
                for o in self.streams[e]:
                    if o.is_dma:
                        continue
                    if o.signal:
                        cnt += 1
                        o.sem = sems[e]
                        o.semval = cnt
            for e in DMAQ:
                cnts = [0] * self.n_dma_sems
                i = 0
                for o in self.streams[e]:
                    if not o.is_dma:
                        continue
                    k = i % self.n_dma_sems
                    cnts[k] += 16
                    o.sem = dsems[e][k]
                    o.semval = cnts[k]
                    i += 1
            block = st.enter_context(nc.Block())

            def run(ename, eng):
                waited = {}
                for o in self.streams[ename]:
                    for d in o.deps:
                        key = id(d.sem)
                        if waited.get(key, 0) >= d.semval:
                            continue
                        eng.wait_ge(d.sem, d.semval)
                        waited[key] = d.semval
                    ins = o.fn(eng)
                    if o.signal:
                        ins.then_inc(o.sem, 16 if o.is_dma else 1)
                if ename == "sp":
                    for o in self.final_ops:
                        key = id(o.sem)
                        if waited.get(key, 0) >= o.semval:
                            continue
                        eng.wait_ge(o.sem, o.semval)
                        waited[key] = o.semval

            @block.tensor
            def _(e):
                run("pe", e)

            @block.scalar
            def _(e):
                run("act", e)

            @block.vector
            def _(e):
                run("dve", e)

            @block.gpsimd
            def _(e):
                run("pool", e)

            @block.sync
            def _(e):
                run("sp", e)


class StopBuild(Exception):
    pass


STOP = None
DUMPS = []


def checkpoint(name):
    if STOP == name:
        raise StopBuild(name)


class Rot:
    def __init__(self, items):
        self.items = items
        self.i = 0

    def next(self):
        x = self.items[self.i % len(self.items)]
        self.i += 1
        return x


def rs_of(r):
    return min(max(r - 4, 0), ROWS - 8)


def build_program(depth=DEPTH, do_S=True, do_P=True, s_tiles=4):
    nc = bass.Bass("TRN2", target_bir_lowering=False)
    S = Sched(nc)

    def din(name, shape, dt=F32):
        return nc.dram_tensor(name, list(shape), dt, kind="ExternalInput").ap()

    def dout(name, shape):
        return nc.dram_tensor(name, list(shape), F32, kind="ExternalOutput").ap()

    xs_d = din("xs", [2048, D])
    xp_d = din("xp", [512, D])
    ck_d = din("ck", [DEPTH, NH, 256, 64])
    cv_d = din("cv", [DEPTH, NH, 256, 64])
    condT_d = din("condT", [128, 8, 2])
    bada_d = din("bada", [128, DEPTH, 48])
    vec8_d = din("vec8", [128, DEPTH, 5, 8])
    fing_d = din("fing", [128, 8])
    dww_d = din("dww", [128, DEPTH, 8, 31])
    scw_d = din("scw", [128, DEPTH, 8, 3])
    rpbR_d = din("rpbR", [DEPTH, NH, 128, 15])
    cmask_d = din("cmask", [128, 64])
    identf_d = din("identf", [128, 128])
    jrev_d = din("jrev", [128, 128])
    w_ada_d = din("w_ada", [DEPTH, D, 6 * D])
    w_in_d = din("w_in", [DEPTH, D, D_IN])
    sq_d = {n: din(n, [DEPTH, D, D]) for n in ("conv_pw_w", "sc_out_w", "na_out_w", "w_o")}
    wg_d = din("ffn_w_gate", [DEPTH, D, D_FF])
    wu_d = din("ffn_w_up", [DEPTH, D, D_FF])
    wd_d = din("ffn_w_down", [DEPTH, D_FF, D])
    ys_d = dout("ys", [2048, D])
    yp_d = dout("yp", [512, D])
    nk_d = dout("nk", [2, DEPTH, NH, 256, 64])
    nv_d = dout("nv", [2, DEPTH, NH, 256, 64])

    blocks = {}
    order_layer = []

    def add_block(bid, kc, nb, casts):
        blocks[bid] = dict(idx=len(blocks), kc=kc, nb=nb, casts=casts)

    for l in range(depth):
        wa = w_ada_d[l].rearrange("(kc p) n -> p kc n", p=128)
        wi = w_in_d[l].rearrange("(kc p) n -> p kc n", p=128)
        for b in range(12):
            add_block((l, "ada", b), 8, 512, [(0, 512, wa[:, :, 512 * b:512 * b + 512])])
        for b in range(22):
            add_block((l, "win", b), 8, 512, [(0, 512, wi[:, :, 512 * b:512 * b + 512])])
        for n, key in (("conv_pw_w", "pw"), ("sc_out_w", "sco"), ("na_out_w", "nao"), ("w_o", "wo")):
            ws_ = sq_d[n][l].rearrange("(kc p) n -> p kc n", p=128)
            for b in range(2):
                add_block((l, key, b), 8, 512, [(0, 512, ws_[:, :, 512 * b:512 * b + 512])])
        g_ = wg_d[l].rearrange("(kc p) n -> p kc n", p=128)
        u_ = wu_d[l].rearrange("(kc p) n -> p kc n", p=128)
        for b in range(11):
            add_block((l, "ffn", b), 8, 512, [(0, 256, g_[:, :, 256 * b:256 * b + 256]),
                                             (256, 256, u_[:, :, 256 * b:256 * b + 256])])
        d_ = wd_d[l].rearrange("(kc p) n -> p kc n", p=128)
        for b in range(8):
            add_block((l, "dn", b), 22, 128, [(0, 128, d_[:, :, 128 * b:128 * b + 128])])
    wsc_d = nc.dram_tensor("wsc", [len(blocks), 128, SLOT_E], BF16, kind="Internal").ap()
    Gs_d = nc.dram_tensor("Gs", [DEPTH, 128, 8 * 15 * 64], BF16, kind="Internal").ap()
    KTs_d = nc.dram_tensor("KTs", [DEPTH, 128, 8 * 256], BF16, kind="Internal").ap()
    bGs = [Buf("Gs%d" % l) for l in range(DEPTH)]
    bKTs = [Buf("KTs%d" % l) for l in range(DEPTH)]
    bscr = {bid: Buf("scr%d" % blk["idx"]) for bid, blk in blocks.items()}

    def tile_blocks(l):
        seq = []
        for hg in range(2):
            seq += [(l, "win", 4 + hg), (l, "win", 6 + hg), (l, "win", 8 + hg)]
        for h in range(2):
            seq += [(l, "nao", h), (l, "win", 18 + h)]
        for h in range(2):
            seq += [(l, "win", 0 + h), (l, "win", 2 + h)]
        for h in range(2):
            seq += [(l, "pw", h), (l, "win", 16 + h)]
        for h in range(2):
            seq += [(l, "win", 12 + h), (l, "win", 14 + h)]
        seq += [(l, "win", 10), (l, "win", 11)]
        for h in range(2):
            seq += [(l, "sco", h), (l, "win", 20 + h)]
        seq += [(l, "wo", 0), (l, "wo", 1)]
        seq += [(l, "ffn", b) for b in range(11)]
        seq += [(l, "dn", b) for b in range(8)]
        return seq

    def ada_blocks(l):
        return [(l, "ada", b) for b in range(12)]

    with contextlib.ExitStack() as st:
        def sb(name, shape, dt):
            return st.enter_context(nc.sbuf_tensor(name, list(shape), dt))

        xT = sb("xT", [128, 8, 2048], F32)
        uw = sb("uw", [128, 8, UW], BF16)
        ring = [sb("ring%d" % i, [128, SLOT_E], BF16) for i in range(NSLOT)]
        G = sb("G", [128, 8, 15, 64], BF16)
        ctxKT = sb("ctxKT", [128, 8, 256], BF16)
        ctxV = sb("ctxV", [128, 2, 1024], BF16)
        yat = sb("yat", [128, 8, 512], BF16)
        mixed = sb("mixed", [128, 8, 512], BF16)
        arena_t = sb("arena", [128, ARENA_BYTES], U8)
        tmpf_t = [sb("tmpf%d" % i, [128, 576], F32) for i in range(4)]
        tmpb_t = [sb("tmpb%d" % i, [128, 576], BF16) for i in range(3)]
        rsd_t = [sb("rsd%d" % i, [128, 512], F32) for i in range(1)]
        modt = sb("modt", [128, DEPTH, 48, 2], F32)
        A1t = sb("A1t", [128, DEPTH, 8, 2], F32)
        A2t = sb("A2t", [128, DEPTH, 8, 2], F32)
        condT = sb("condT_s", [128, 8, 2], F32)
        condS = sb("condS", [128, 8, 2], BF16)
        bada = sb("bada_s", [128, DEPTH, 48], F32)
        vec8 = sb("vec8_s", [128, DEPTH, 5, 8], F32)
        fing = sb("fing_s", [128, 8], F32)
        dww = sb("dww_s", [128, DEPTH, 8, 31], F32)
        scw = sb("scw_s", [128, DEPTH, 8, 3], F32)
        cmask = sb("cmask_s", [128, 64], F32)
        identf = sb("identf_s", [128, 128], F32)
        identb = sb("identb_s", [128, 128], BF16)
        jrevf = sb("jrevf_s", [128, 128], F32)
        jrevb = sb("jrevb_s", [128, 128], BF16)
        onesb = sb("onesb", [128, 128], BF16)
        onesbd = sb("onesbd", [128, 128], BF16)
        ps = [st.enter_context(nc.psum_tensor("ps%d" % i, [128, 512], F32)) for i in range(8)]

        bps = [Buf("ps%d" % i, excl=True) for i in range(8)]
        bx = [[Buf("x%d_%d" % (t, j)) for j in range(8)] for t in range(4)]
        bu = [Buf("u%d" % k) for k in range(8)]
        bslot = [Buf("slot%d" % i) for i in range(NSLOT)]
        bG, bctxK, bctxV = Buf("G"), Buf("ctxK"), Buf("ctxV")
        byat = [Buf("yat%d" % j) for j in range(8)]
        bmix = [Buf("mix%d" % j) for j in range(8)]
        bconst = Buf("const")
        bmod = Buf("mod")
        tmpf = Rot([(t[:], Buf("tmpf%d" % i)) for i, t in enumerate(tmpf_t)])
        tmpb = Rot([(t[:], Buf("tmpb%d" % i)) for i, t in enumerate(tmpb_t)])
        rsd = Rot([(t[:], Buf("rsd%d" % i)) for i, t in enumerate(rsd_t)])
        bank_main = Rot([0, 1])
        bank_aux = Rot([2, 3])
        bank_s = Rot([4, 5])
        BN, BD = 6, 7

        def MM(out, lhsT, rhs, start, stop, reads, writes, skip=False):
            S.op("pe", lambda e: e.matmul(out, lhsT=lhsT, rhs=rhs, start=start, stop=stop, skip_group_check=skip),
                 reads=reads, writes=writes)

        def TR(out, in_, ident, reads, writes):
            S.op("pe", lambda e: e.transpose(out=out, in_=in_, identity=ident), reads=reads, writes=writes)

        def ACT(out, in_, func, reads, writes, bias=None, scale=None):
            kw = {}
            if bias is not None:
                kw["bias"] = bias
            if scale is not None:
                kw["scale"] = scale
            S.op("act", lambda e: e.activation(out=out, in_=in_, func=func, **kw), reads=reads, writes=writes)

        def TT(out, in0, in1, op, reads, writes, eng="dve"):
            S.op(eng, lambda e: e.tensor_tensor(out=out, in0=in0, in1=in1, op=op), reads=reads, writes=writes)

        def TS(out, in0, s1, s2, op0, op1, reads, writes, eng="dve"):
            if op1 is None:
                S.op(eng, lambda e: e.tensor_scalar(out=out, in0=in0, scalar1=s1, scalar2=None, op0=op0),
                     reads=reads, writes=writes)
            else:
                S.op(eng, lambda e: e.tensor_scalar(out=out, in0=in0, scalar1=s1, scalar2=s2, op0=op0, op1=op1),
                     reads=reads, writes=writes)

        def STT(out, in0, scalar, in1, op0, op1, reads, writes):
            S.op("dve", lambda e: e.scalar_tensor_tensor(out=out, in0=in0, scalar=scalar, in1=in1, op0=op0, op1=op1),
                 reads=reads, writes=writes)

        def CPY(eng, out, in_, reads, writes):
            if eng == "act":
                ACT(out, in_, AF.Identity, reads, writes)
            else:
                S.op(eng, lambda e: e.tensor_copy(out=out, in_=in_), reads=reads, writes=writes)

        def MEMSET(eng, ap, val, writes):
            S.op(eng, lambda e: e.memset(ap, val), writes=writes)

        def DMA(eng, out, in_, reads, writes, final=False):
            return S.op(eng, lambda e: e.dma_start(out=out, in_=in_), reads=reads, writes=writes, dma=True, final=final)

        class Arena:
            def __init__(self):
                self.off = 0

            def reset(self):
                self.off = 0

            def alloc(self, name, n, dt):
                esz = 4 if dt == F32 else 2
                nb = (n * esz + 31) // 32 * 32
                lo, hi = self.off, self.off + nb
                assert hi <= ARENA_BYTES, (name, hi)
                self.off = hi
                ap = arena_t[:, lo:lo + n * esz].bitcast(dt)
                return ap, S.arena_buf(name, lo, hi)

        arena = Arena()

        class WS:
            def __init__(self):
                self.seq = []
                self.pos = 0
                self.issued = 0
                self.cast_done = set()
                self.cast_queue = []

            def cast(self, bid):
                if bid in self.cast_done:
                    return
                self.cast_done.add(bid)
                blk = blocks[bid]
                dst = wsc_d[blk["idx"]][:, 0:blk["kc"] * blk["nb"]].rearrange("p (k n) -> p k n", n=blk["nb"])
                for (c0, ncol, src) in blk["casts"]:
                    DMA("pool", dst[:, :, c0:c0 + ncol], src, [], [bscr[bid]])

            def cast_some(self, n):
                if S.plan:
                    return
                while n > 0 and self.cast_queue:
                    self.cast(self.cast_queue.pop(0))
                    n -= 1

            def _issue(self, n):
                while self.issued <= n and self.issued < len(self.seq):
                    bid = self.seq[self.issued]
                    self.cast(bid)
                    blk = blocks[bid]
                    sl = self.issued % NSLOT
                    ne = blk["kc"] * blk["nb"]
                    DMA("sp", ring[sl][:, 0:ne], wsc_d[blk["idx"]][:, 0:ne], [bscr[bid]], [bslot[sl]])
                    self.issued += 1

            def get(self, bid):
                if S.plan:
                    self.seq.append(bid)
                    blk = blocks[bid]
                    return ring[0][:, 0:blk["kc"] * blk["nb"]].rearrange("p (k n) -> p k n", n=blk["nb"]), bslot[0]
                assert self.seq[self.pos] == bid, (self.pos, self.seq[self.pos], bid)
                self._issue(self.pos + NSLOT - 2)
                sl = self.pos % NSLOT
                self.pos += 1
                blk = blocks[bid]
                view = ring[sl][:, 0:blk["kc"] * blk["nb"]].rearrange("p (k n) -> p k n", n=blk["nb"])
                return view, bslot[sl]

        ws = WS()
        wsh = [ws]

        def emit_consts():
            for dst, src in ((condT, condT_d), (bada, bada_d), (vec8, vec8_d), (fing, fing_d), (dww, dww_d),
                             (scw, scw_d), (cmask, cmask_d), (identf, identf_d), (jrevf, jrev_d)):
                DMA("sp", dst[:], src, [], [bconst])
            CPY("dve", identb[:], identf[:], [bconst], [bconst])
            CPY("dve", jrevb[:], jrevf[:], [bconst], [bconst])
            MEMSET("dve", onesb[:], 1.0, [bconst])
            MEMSET("dve", onesbd[:], 1.0, [bconst])
            MEMSET("dve", onesbd[0:64, 64:128], 0.0, [bconst])
            MEMSET("dve", onesbd[64:128, 0:64], 0.0, [bconst])
            for k in range(8):
                MEMSET("dve", uw[:, k, :], 0.0, [bu[k]])
            ACT(condS[:], condT[:], AF.Silu, [bconst], [bconst])

        def proj(pb, ncols, wview, wcols, rhs_fn, nk, wbuf, rhs_bufs_fn, pcol0=0):
            for k in range(nk):
                MM(ps[pb][:, pcol0:pcol0 + ncols], wview[:, k, wcols], rhs_fn(k), k == 0, k == nk - 1,
                   [wbuf] + rhs_bufs_fn(k), [bps[pb]])

        def adaln(l):
            pb = bank_main.next()
            for b in range(12):
                W, wb = ws.get((l, "ada", b))
                for jj in range(4):
                    ch = 4 * b + jj
                    for k in range(8):
                        MM(ps[pb][:, 2 * ch:2 * ch + 2], W[:, k, jj * 128:(jj + 1) * 128], condS[:, k, :], k == 0, k == 7,
                           [wb, bconst], [bps[pb]])
            pv = ps[pb][:, 0:96].rearrange("p (a b) -> p a b", b=2)
            for c in range(2):
                TT(modt[:, l, :, c], pv[:, :, c], bada[:, l, :], ALU.add, [bps[pb], bconst], [bmod])
            for c in range(2):
                STT(A1t[:, l, :, c], modt[:, l, 8:16, c], 1.0, vec8[:, l, 0, :], ALU.add, ALU.mult, [bmod, bconst], [bmod])
                STT(A2t[:, l, :, c], modt[:, l, 32:40, c], 1.0, vec8[:, l, 1, :], ALU.add, ALU.mult, [bmod, bconst], [bmod])

        def rms_modulate(n, xsrc, xbufs, dst, dbufs, scale_ap, shift_ap):
            pb = bank_s.next()
            for j in range(8):
                sq, sqb = tmpb.next()
                ACT(sq[:, 0:n], xsrc(j), AF.Square, [xbufs(j)], [sqb])
                MM(ps[pb][:, 0:n], onesb[:], sq[:, 0:n], j == 0, j == 7, [bconst, sqb], [bps[pb]])
            ln_, lnb = tmpf.next()
            ACT(ln_[:, 0:n], ps[pb][:, 0:n], AF.Ln, [bps[pb]], [lnb], bias=EPS, scale=1.0 / D)
            rstd, rsb = rsd.next()
            ACT(rstd[:, 0:n], ln_[:, 0:n], AF.Exp, [lnb], [rsb], scale=-0.5)
            for j in range(8):
                t, tb = tmpf.next()
                TT(t[:, 0:n], xsrc(j), rstd[:, 0:n], ALU.mult, [xbufs(j), rsb], [tb])
                sh = shift_ap(j)
                ACT(dst(j), t[:, 0:n], AF.Identity, [tb, bmod, bconst], [dbufs(j)], bias=(sh if sh is not None else 0.0),
                    scale=scale_ap(j))

        def mix_out(l, cnd, ysrc, ybufs, wname, gate_blk, first):
            for half in range(2):
                W, wb = ws.get((l, wname, half))
                Wg, wgb = ws.get((l, "win", gate_blk + half))
                for jj in range(4):
                    j = 4 * half + jj
                    pb = bank_main.next()
                    proj(pb, 512, W, slice(jj * 128, (jj + 1) * 128), ysrc, 8, wb, lambda k: [ybufs(k)])
                    gb = bank_aux.next()
                    proj(gb, 512, Wg, slice(jj * 128, (jj + 1) * 128), lambda k: uw[:, k, LEFT:LEFT + 512], 8, wgb,
                         lambda k: [bu[k]])
                    g, gbuf = tmpf.next()
                    ACT(g[:, 0:512], ps[gb][:, 0:512], AF.Sigmoid, [bps[gb]], [gbuf])
                    if first:
                        TT(mixed[:, j, :], ps[pb][:, 0:512], g[:, 0:512], ALU.mult, [bps[pb], gbuf], [bmix[j]])
                    else:
                        t2, t2b = tmpf.next()
                        TT(t2[:, 0:512], ps[pb][:, 0:512], g[:, 0:512], ALU.mult, [bps[pb], gbuf], [t2b])
                        TT(mixed[:, j, :], mixed[:, j, :], t2[:, 0:512], ALU.add, [bmix[j], t2b], [bmix[j]])

        def load_ctxv(l):
            for c in range(2):
                DMA("pool", ctxV[:, c, :].rearrange("p (h d) -> p h d", d=64),
                    cv_d[l, :, c * 128:(c + 1) * 128, :].rearrange("h s d -> s h d"), [], [bctxV])

        def tables_pre():
            for l in reversed(range(depth)):
                arena.reset()
                hst = [arena.alloc("hstf%d" % i, 960, F32) for i in range(2)]
                kstf = [arena.alloc("kstf%d" % c, 16 * 64, F32) for c in range(2)]
                for hp in range(8):
                    h_, hb_ = hst[hp % 2]
                    for s in range(2):
                        src = bass.AP(rpbR_d.tensor, ((l * NH) + 2 * hp + s) * 15 * 128, [[15, 64], [1, 960]])
                        DMA("act", h_[64 * s:64 * s + 64, :], src, [], [hb_])
                    cm = cmask[:]
                    for w0 in (0, 32):
                        pb = bank_aux.next()
                        MM(ps[pb][:, 0:480], jrevf[:], h_[:, w0 * 15:(w0 + 32) * 15], True, True, [bconst, hb_], [bps[pb]])
                        cmb = bass.AP(cm.tensor, cm.offset + w0, [cm.ap[0], [0, 15], [1, 32]])
                        TT(G[:, hp, :, w0:w0 + 32], ps[pb][:, 0:480].rearrange("p (w e) -> p e w", e=15), cmb, ALU.add,
                           [bps[pb], bconst], [bG])
                for c in range(2):
                    DMA("act", kstf[c][0].rearrange("p (h d) -> p h d", d=64),
                        ck_d[l, :, c * 128:(c + 1) * 128, :].rearrange("h s d -> s h d"), [], [kstf[c][1]])
                for hp in range(8):
                    pb = bank_aux.next()
                    for c in range(2):
                        TR(ps[pb][:, c * 128:(c + 1) * 128], kstf[c][0][:, 2 * hp * 64:(2 * hp + 2) * 64], identf[:],
                           [kstf[c][1], bconst], [bps[pb]])
                    CPY("dve", ctxKT[:, hp, :], ps[pb][:, 0:256], [bps[pb]], [bctxK])
                if l > 0:
                    DMA("act", Gs_d[l], G[:].rearrange("p a b c -> p (a b c)"), [bG], [bGs[l]])
                    DMA("act", KTs_d[l], ctxKT[:].rearrange("p a b -> p (a b)"), [bctxK], [bKTs[l]])
            load_ctxv(0)

        def build_layer_tables(l):
            if l == 0:
                return
            DMA("sp", G[:].rearrange("p a b c -> p (a b c)"), Gs_d[l], [bGs[l]], [bG])
            DMA("sp", ctxKT[:].rearrange("p a b -> p (a b)"), KTs_d[l], [bKTs[l]], [bctxK])
            load_ctxv(l)

        def norm1(l, kind, ti):
            cnd = 0 if kind == "S" else 1
            own0 = 512 * ti
            A1 = lambda j: A1t[:, l, j, cnd:cnd + 1]
            mod = lambda idx, j: modt[:, l, idx * 8 + j, cnd:cnd + 1]
            if kind == "S" and ti > 0:
                for k in range(8):
                    CPY("act", uw[:, k, LEFT - 256:LEFT], uw[:, k, LEFT + 256:LEFT + 512], [bu[k]], [bu[k]])
            ranges = [(ti, own0, 512, LEFT)]
            if kind == "S" and ti < 3:
                ranges.append((ti + 1, own0 + 512, 256, LEFT + 512))
            for (tsrc, xc, n, uc) in ranges:
                rms_modulate(n, lambda j: xT[:, j, xc:xc + n], lambda j: bx[tsrc][j],
                             lambda j: uw[:, j, uc:uc + n], lambda j: bu[j], A1, lambda j: mod(0, j))

        def tile_prog(l, kind, ti, is_last, hoist):
            cnd = 0 if kind == "S" else 1
            own0 = 512 * ti
            xb_own = bx[ti]
            if kind == "S":
                r0 = 8 * ti
                kmin, kmax = rs_of(r0), rs_of(r0 + 7) + 7
                nrows = kmax - kmin + 1
                nseg, L = 1, 512
            else:
                nseg, L = 2, 256
            A1 = lambda j: A1t[:, l, j, cnd:cnd + 1]
            A2 = lambda j: A2t[:, l, j, cnd:cnd + 1]
            mod = lambda idx, j: modt[:, l, idx * 8 + j, cnd:cnd + 1]
            uown = lambda k: uw[:, k, LEFT:LEFT + 512]
            ubuf = lambda k: [bu[k]]

            if STOP == "norm1":
                DUMPS.append(("uw", uw[:, :, LEFT:LEFT + 512], bu))
                DUMPS.append(("modt", modt[:, 0], [bmod]))
                DUMPS.append(("A1t", A1t[:, 0], [bmod]))
                DUMPS.append(("xT", xT[:, :, 0:512], bx[0]))
            checkpoint("norm1")
            for hg in range(2):
                arena.reset()
                KT = [arena.alloc("KT%d" % jj, UW, BF16) for jj in range(4)]
                QT = [arena.alloc("QT%d" % jj, 512, BF16) for jj in range(4)]
                VA = [arena.alloc("VA%d" % m, 512, BF16) for m in range(8)]
                VS = [arena.alloc("VS%d" % m, 512, BF16) for m in range(8)]
                W, wb = ws.get((l, "win", 4 + hg))
                for jj in range(4):
                    pb = bank_main.next()
                    proj(pb, 512, W, slice(jj * 128, (jj + 1) * 128), uown, 8, wb, ubuf)
                    CPY("act", QT[jj][0], ps[pb][:, 0:512], [bps[pb]], [QT[jj][1]])
                checkpoint("q")
                W, wb = ws.get((l, "win", 6 + hg))
                if kind == "S":
                    c0, c1 = LEFT + (kmin - r0) * 64, LEFT + (kmax + 1 - r0) * 64
                else:
                    c0, c1 = LEFT, LEFT + 512
                pieces = []
                c = c0
                while c < c1:
                    n = min(512, c1 - c)
                    pieces.append((c, n))
                    c += n
                for jj in range(4):
                    for (c, n) in pieces:
                        pb = bank_main.next()
                        proj(pb, n, W, slice(jj * 128, (jj + 1) * 128), lambda k, c=c, n=n: uw[:, k, c:c + n], 8, wb, ubuf)
                        CPY("dve", KT[jj][0][:, c:c + n], ps[pb][:, 0:n], [bps[pb]], [KT[jj][1]])
                checkpoint("k")
                if kind == "P":
                    for tc in range(4):
                        pb = bank_aux.next()
                        for k in range(8):
                            MM(ps[pb][:, 0:512], uw[:, k, LEFT + 128 * tc:LEFT + 128 * tc + 128], W[:, k, :], k == 0, k == 7,
                               [wb, bu[k]], [bps[pb]])
                        stg, stb = tmpf.next()
                        CPY("act", stg[:, 0:512], ps[pb][:, 0:512], [bps[pb]], [stb])
                        dst = nk_d[tc // 2, l, 8 * hg:8 * hg + 8, (tc % 2) * 128:(tc % 2) * 128 + 128, :].rearrange("h s d -> s h d")
                        DMA("pool", dst, stg[:, 0:512].rearrange("p (h d) -> p h d", d=64), [stb], [], final=True)
                checkpoint("knk")
                W, wb = ws.get((l, "win", 8 + hg))
                vjobs = []
                if kind == "S":
                    nA = (nrows + 1) // 2
                    for m in range(nA):
                        vjobs.append((VA[m], LEFT + (kmin + 2 * m - r0) * 64, None))
                else:
                    for tc in range(4):
                        vjobs.append((VA[tc], LEFT + 128 * tc, tc))
                for (vdst, col, tc) in vjobs:
                    pb = bank_main.next()
                    for k in range(8):
                        MM(ps[pb][:, 0:512], uw[:, k, col:col + 128], W[:, k, :], k == 0, k == 7, [wb, bu[k]], [bps[pb]])
                    CPY("dve", vdst[0], ps[pb][:, 0:512], [bps[pb]], [vdst[1]])
                    if tc is None:
                        m_ = VA.index(vdst)
                        CPY("act", VS[m_][0][0:64, :], vdst[0][64:128, :], [vdst[1]], [VS[m_][1]])
                        CPY("dve", VS[m_][0][64:128, :], vdst[0][0:64, :], [vdst[1]], [VS[m_][1]])
                    if tc is not None:
                        stg, stb = tmpf.next()
                        CPY("act", stg[:, 0:512], ps[pb][:, 0:512], [bps[pb]], [stb])
                        dst = nv_d[tc // 2, l, 8 * hg:8 * hg + 8, (tc % 2) * 128:(tc % 2) * 128 + 128, :].rearrange("h s d -> s h d")
                        DMA("pool", dst, stg[:, 0:512].rearrange("p (h d) -> p h d", d=64), [stb], [], final=True)

                checkpoint("qkv")
                jobs = []
                for jj in range(4):
                    hp = 4 * hg + jj
                    kt, ktb = KT[jj]
                    qt, qtb = QT[jj]
                    first = len(jobs)
                    if kind == "S":
                        for q in range(nrows):
                            kr = kmin + q
                            rr = [r for r in range(r0, r0 + 8) if rs_of(r) <= kr <= rs_of(r) + 7]
                            ra, rb = rr[0], rr[-1]
                            nr = rb - ra + 1
                            nq = 64 * nr
                            qc = (ra - r0) * 64
                            kcol = LEFT + (kr - r0) * 64
                            e0 = ra - kr + 7
                            if q % 2 == 0:
                                vA, vB = VA[q // 2], VS[q // 2]
                            else:
                                vA, vB = VS[(q - 1) // 2], VA[(q - 1) // 2]
                            jobs.append(dict(kind="loc", hp=hp, jj=jj, nq=nq, qc=qc, kcol=kcol, e0=e0, nr=nr, vA=vA, vB=vB,
                                             kt=kt, ktb=ktb, qt=qt, qtb=qtb))
                        for s_ in range(2):
                            for c in range(2):
                                jobs.append(dict(kind="den", hp=hp, jj=jj, s=s_, nq=512, qc=0, qt=qt, qtb=qtb,
                                                 klhs=ctxKT[64 * s_:64 * s_ + 64, hp, c * 128:(c + 1) * 128], kbuf=bctxK,
                                                 vlhs=ctxV[:, c, (2 * hp + s_) * 64:(2 * hp + s_) * 64 + 64], vbuf=bctxV))
                    else:
                        for sq in range(2):
                            for s_ in range(2):
                                for c in range(2):
                                    kc0 = LEFT + sq * 256 + c * 128
                                    jobs.append(dict(kind="den", hp=hp, jj=jj, s=s_, nq=256, qc=sq * 256, qt=qt, qtb=qtb,
                                                     klhs=kt[64 * s_:64 * s_ + 64, kc0:kc0 + 128], kbuf=ktb,
                                                     vlhs=VA[2 * sq + c][0][:, jj * 128 + 64 * s_:jj * 128 + 64 * s_ + 64],
                                                     vbuf=VA[2 * sq + c][1]))
                    jobs[first]["first"] = True
                    jobs[-1]["last"] = True
                s_banks = Rot([4, 5, 0, 1])
                nd_pairs = Rot([(6, 7), (2, 3)])
                cur_nd = {}

                def emitS(jb):
                    if jb.get("first"):
                        bn, bd = nd_pairs.next()
                        cur_nd[jb["hp"]] = (bn, bd)
                        MEMSET("dve", ps[bn][:], 0.0, [bps[bn]])
                        MEMSET("dve", ps[bd][:], 0.0, [bps[bd]])
                    sbk = s_banks.next()
                    jb["sbk"] = sbk
                    nq, qc, qt, qtb = jb["nq"], jb["qc"], jb["qt"], jb["qtb"]
                    if jb["kind"] == "loc":
                        for s_ in range(2):
                            MM(ps[sbk][64 * s_:64 * s_ + 64, 0:nq], jb["kt"][64 * s_:64 * s_ + 64, jb["kcol"]:jb["kcol"] + 64],
                               qt[64 * s_:64 * s_ + 64, qc:qc + nq], True, True, [jb["ktb"], qtb], [bps[sbk]])
                    else:
                        s_ = jb["s"]
                        MM(ps[sbk][:, 0:nq], jb["klhs"], qt[64 * s_:64 * s_ + 64, qc:qc + nq], True, True, [jb["kbuf"], qtb], [bps[sbk]])

                def emitE(jb):
                    sbk, nq = jb["sbk"], jb["nq"]
                    p, pbuf = tmpb.next()
                    if jb["kind"] == "loc":
                        t, tb = tmpf.next()
                        STT(t[:, 0:nq], ps[sbk][:, 0:nq], ATT_SCALE,
                            G[:, jb["hp"], jb["e0"]:jb["e0"] + jb["nr"], :].rearrange("p a b -> p (a b)"),
                            ALU.mult, ALU.add, [bps[sbk], bG], [tb])
                        ACT(p[:, 0:nq], t[:, 0:nq], AF.Exp, [tb], [pbuf])
                    else:
                        ACT(p[:, 0:nq], ps[sbk][:, 0:nq], AF.Exp, [bps[sbk]], [pbuf], scale=ATT_SCALE)
                    jb["p"] = (p, pbuf)

                def emitPV(jb):
                    bn, bd = cur_nd[jb["hp"]]
                    nq, qc, jj = jb["nq"], jb["qc"], jb["jj"]
                    p, pbuf = jb["p"]
                    if jb["kind"] == "loc":
                        for s_, v in ((0, jb["vA"]), (1, jb["vB"])):
                            MM(ps[bn][64 * s_:64 * s_ + 64, qc:qc + nq],
                               v[0][64 * s_:64 * s_ + 64, jj * 128 + 64 * s_:jj * 128 + 64 * s_ + 64],
                               p[64 * s_:64 * s_ + 64, 0:nq], False, True, [v[1], pbuf], [bps[bn]], skip=True)
                        MM(ps[bd][:, qc:qc + nq], onesbd[:], p[:, 0:nq], False, True, [bconst, pbuf], [bps[bd]], skip=True)
                    else:
                        s_ = jb["s"]
                        MM(ps[bn][64 * s_:64 * s_ + 64, qc:qc + nq], jb["vlhs"], p[:, 0:nq], False, True, [jb["vbuf"], pbuf],
                           [bps[bn]], skip=True)
                        MM(ps[bd][64 * s_:64 * s_ + 64, qc:qc + nq], onesb[:, 0:64], p[:, 0:nq], False, True, [bconst, pbuf],
                           [bps[bd]], skip=True)
                    if jb.get("last"):
                        hp = jb["hp"]
                        lnd, lndb = tmpf.next()
                        ACT(lnd[:, 0:512], ps[bd][:, 0:512], AF.Ln, [bps[bd]], [lndb])
                        rec, recb = tmpf.next()
                        ACT(rec[:, 0:512], lnd[:, 0:512], AF.Exp, [lndb], [recb], scale=-1.0)
                        TT(yat[:, hp, :], ps[bn][:, 0:512], rec[:, 0:512], ALU.mult, [bps[bn], recb], [byat[hp]])

                LA = 3
                for i in range(min(LA, len(jobs))):
                    emitS(jobs[i])
                for i in range(len(jobs)):
                    emitE(jobs[i])
                    if i + LA < len(jobs):
                        emitS(jobs[i + LA])
                    emitPV(jobs[i])

            if STOP == "attn":
                DUMPS.append(("yat", yat[:], byat))
            checkpoint("attn")
            mix_out(l, cnd, lambda k: yat[:, k, :], lambda k: byat[k], "nao", 18, True)
            if STOP in ("nao", "conv", "sc"):
                DUMPS.append(("mixed", mixed[:], bmix))
            checkpoint("nao")

            arena.reset()
            SEG = L + 30
            agl = [arena.alloc("agl%d" % j, 576, BF16) for j in range(8)]
            hb = [(agl[j][0][:, 0:512], agl[j][1]) for j in range(8)]
            diag = [arena.alloc("diag%d" % i, 31 * 128, BF16) for i in range(2)]
            mu, mub = arena.alloc("mu", 512, F32)
            rstd, rstdb = arena.alloc("rstd", 512, F32)
            murs, mursb = arena.alloc("murs", 512, F32)
            sides = []
            if kind == "S":
                if ti > 0:
                    sides.append((LEFT - 15, 0))
                if ti < 3:
                    sides.append((LEFT + 512, 15 + 512))
            for half in range(2):
                Wa, wab = ws.get((l, "win", 0 + half))
                Wg, wgb = ws.get((l, "win", 2 + half))
                for jj in range(4):
                    j = 4 * half + jj
                    cs = slice(jj * 128, (jj + 1) * 128)
                    a_, ab_ = agl[j]
                    av = a_[:, 0:nseg * SEG].rearrange("p (s c) -> p s c", c=SEG)
                    pa = bank_main.next()
                    proj(pa, 512, Wa, cs, uown, 8, wab, ubuf)
                    pg = bank_aux.next()
                    proj(pg, 512, Wg, cs, uown, 8, wgb, ubuf)
                    sg, sgb = tmpf.next()
                    ACT(sg[:, 0:512], ps[pg][:, 0:512], AF.Sigmoid, [bps[pg]], [sgb])
                    MEMSET("dve", av[:, :, 0:15], 0.0, [ab_])
                    MEMSET("dve", av[:, :, 15 + L:30 + L], 0.0, [ab_])
                    TT(av[:, :, 15:15 + L], ps[pa][:, 0:512].rearrange("p (s c) -> p s c", c=L),
                       sg[:, 0:512].rearrange("p (s c) -> p s c", c=L), ALU.mult, [bps[pa], sgb], [ab_])
                    for (ucol, acol) in sides:
                        ph = bank_s.next()
                        proj(ph, 15, Wa, cs, lambda k, ucol=ucol: uw[:, k, ucol:ucol + 15], 8, wab, ubuf, pcol0=0)
                        proj(ph, 15, Wg, cs, lambda k, ucol=ucol: uw[:, k, ucol:ucol + 15], 8, wgb, ubuf, pcol0=32)
                        sh_, shb = tmpf.next()
                        ACT(sh_[:, 0:15], ps[ph][:, 32:47], AF.Sigmoid, [bps[ph]], [shb])
                        TT(a_[:, acol:acol + 15], ps[ph][:, 0:15], sh_[:, 0:15], ALU.mult, [bps[ph], shb], [ab_])
            idb_ = identb[:]
            for j in range(8):
                a_, ab_ = agl[j]
                av = a_[:, 0:nseg * SEG].rearrange("p (s c) -> p s c", c=SEG)
                dg, dgb = diag[j % 2]
                dgv = dg.rearrange("p (k m) -> p k m", m=128)
                wj = dww[:, l, j, :]
                TT(dgv, bass.AP(idb_.tensor, idb_.offset, [idb_.ap[0], [0, 31], [1, 128]]),
                   bass.AP(wj.tensor, wj.offset, [wj.ap[0], [1, 31], [0, 128]]), ALU.mult, [bconst], [dgb])
                pb = bank_main.next()
                for k in range(31):
                    MM(ps[pb][:, 0:512].rearrange("p (s c) -> p s c", c=L), dgv[:, k, :], av[:, :, k:k + L], k == 0, k == 30,
                       [dgb, ab_], [bps[pb]])
                ACT(hb[j][0], ps[pb][:, 0:512], AF.Identity, [bps[pb], bconst], [hb[j][1]], bias=vec8[:, l, 2, j:j + 1])
            p0, p1 = bank_s.next(), bank_s.next()
            for j in range(8):
                sq, sqb = tmpb.next()
                ACT(sq[:, 0:512], hb[j][0], AF.Square, [hb[j][1]], [sqb])
                MM(ps[p0][:, 0:512], onesb[:], hb[j][0], j == 0, j == 7, [bconst, hb[j][1]], [bps[p0]])
                MM(ps[p1][:, 0:512], onesb[:], sq[:, 0:512], j == 0, j == 7, [bconst, sqb], [bps[p1]])
            ACT(mu, ps[p0][:, 0:512], AF.Identity, [bps[p0]], [mub], scale=1.0 / D)
            msq, msqb = tmpf.next()
            TT(msq[:, 0:512], mu, mu, ALU.mult, [mub], [msqb])
            var, varb = tmpf.next()
            STT(var[:, 0:512], ps[p1][:, 0:512], 1.0 / D, msq[:, 0:512], ALU.mult, ALU.subtract, [bps[p1], msqb], [varb])
            lnv, lnvb = tmpf.next()
            ACT(lnv[:, 0:512], var[:, 0:512], AF.Ln, [varb], [lnvb], bias=EPS)
            ACT(rstd, lnv[:, 0:512], AF.Exp, [lnvb], [rstdb], scale=-0.5)
            TT(murs, mu, rstd, ALU.mult, [mub, rstdb], [mursb])
            for j in range(8):
                t, tb = tmpf.next()
                TT(t[:, 0:512], hb[j][0], rstd, ALU.mult, [hb[j][1], rstdb], [tb])
                t2, t2b = tmpf.next()
                TT(t2[:, 0:512], t[:, 0:512], murs, ALU.subtract, [tb, mursb], [t2b])
                ACT(hb[j][0], t2[:, 0:512], AF.Silu, [t2b, bconst], [hb[j][1]], bias=vec8[:, l, 4, j:j + 1],
                    scale=vec8[:, l, 3, j:j + 1])
            mix_out(l, cnd, lambda k: hb[k][0], lambda k: hb[k][1], "pw", 16, False)
            checkpoint("conv")

            arena.reset()
            SEG2 = L + 2
            cx = [arena.alloc("cx%d" % j, 520, BF16) for j in range(8)]
            scy = [arena.alloc("scy%d" % j, 512, BF16) for j in range(8)]
            sides2 = []
            if kind == "S":
                if ti > 0:
                    sides2.append((LEFT - 1, 0))
                if ti < 3:
                    sides2.append((LEFT + 512, 1 + 512))
            for half in range(2):
                Wc, wcb = ws.get((l, "win", 12 + half))
                Wx, wxb = ws.get((l, "win", 14 + half))
                for jj in range(4):
                    j = 4 * half + jj
                    cs = slice(jj * 128, (jj + 1) * 128)
                    c_, cb_ = cx[j]
                    cvw = c_[:, 0:nseg * SEG2].rearrange("p (s c) -> p s c", c=SEG2)
                    pc_ = bank_main.next()
                    proj(pc_, 512, Wc, cs, uown, 8, wcb, ubuf)
                    px = bank_aux.next()
                    proj(px, 512, Wx, cs, uown, 8, wxb, ubuf)
                    t, tb = tmpf.next()
                    CPY("act", t[:, 0:512], ps[pc_][:, 0:512], [bps[pc_]], [tb])
                    MEMSET("dve", cvw[:, :, 0:1], 0.0, [cb_])
                    MEMSET("dve", cvw[:, :, 1 + L:2 + L], 0.0, [cb_])
                    TT(cvw[:, :, 1:1 + L], t[:, 0:512].rearrange("p (s c) -> p s c", c=L),
                       ps[px][:, 0:512].rearrange("p (s c) -> p s c", c=L), ALU.mult, [tb, bps[px]], [cb_])
                    for (ucol, ccol) in sides2:
                        ph = bank_s.next()
                        proj(ph, 1, Wc, cs, lambda k, ucol=ucol: uw[:, k, ucol:ucol + 1], 8, wcb, ubuf, pcol0=0)
                        proj(ph, 1, Wx, cs, lambda k, ucol=ucol: uw[:, k, ucol:ucol + 1], 8, wxb, ubuf, pcol0=8)
                        th, thb = tmpf.next()
                        CPY("act", th[:, 0:1], ps[ph][:, 0:1], [bps[ph]], [thb])
                        TT(c_[:, ccol:ccol + 1], th[:, 0:1], ps[ph][:, 8:9], ALU.mult, [thb, bps[ph]], [cb_])
            for half in range(2):
                Wb, wbb = ws.get((l, "win", 10 + half))
                for jj in range(4):
                    j = 4 * half + jj
                    c_, cb_ = cx[j]
                    cvw = c_[:, 0:nseg * SEG2].rearrange("p (s c) -> p s c", c=SEG2)
                    c3, c3b = tmpf.next()
                    c3v = c3[:, 0:512].rearrange("p (s c) -> p s c", c=L)
                    TS(c3v, cvw[:, :, 0:L], scw[:, l, j, 0:1], None, ALU.mult, None, [cb_, bconst], [c3b])
                    for k in (1, 2):
                        STT(c3v, cvw[:, :, k:k + L], scw[:, l, j, k:k + 1], c3v, ALU.mult, ALU.add, [cb_, bconst, c3b], [c3b])
                    pb = bank_main.next()
                    proj(pb, 512, Wb, slice(jj * 128, (jj + 1) * 128), uown, 8, wbb, ubuf)
                    TT(scy[j][0], ps[pb][:, 0:512], c3[:, 0:512], ALU.mult, [bps[pb], c3b], [scy[j][1]])
            mix_out(l, cnd, lambda k: scy[k][0], lambda k: scy[k][1], "sco", 20, False)
            checkpoint("sc")

            hoist()
            for half in range(2):
                W, wb = ws.get((l, "wo", half))
                for jj in range(4):
                    j = 4 * half + jj
                    pb = bank_main.next()
                    proj(pb, 512, W, slice(jj * 128, (jj + 1) * 128), lambda k: mixed[:, k, :], 8, wb, lambda k: [bmix[k]])
                    xo = xT[:, j, own0:own0 + 512]
                    STT(xo, ps[pb][:, 0:512], mod(2, j), xo, ALU.mult, ALU.add, [bps[pb], bmod, xb_own[j]], [xb_own[j]])

            if STOP in ("wo", "ffn"):
                DUMPS.append(("xT", xT[:, :, own0:own0 + 512], xb_own))
            checkpoint("wo")
            arena.reset()
            u2 = [arena.alloc("u2_%d" % k, 512, BF16) for k in range(8)]
            hm = [arena.alloc("hm%d" % c, 512, BF16) for c in range(NFF)]
            rms_modulate(512, lambda j: xT[:, j, own0:own0 + 512], lambda j: xb_own[j],
                         lambda j: u2[j][0], lambda j: u2[j][1], A2, lambda j: mod(3, j))
            for b in range(11):
                W, wb = ws.get((l, "ffn", b))
                for c2 in range(2):
                    hc = 2 * b + c2
                    pg = bank_main.next()
                    proj(pg, 512, W, slice(c2 * 128, (c2 + 1) * 128), lambda k: u2[k][0], 8, wb, lambda k: [u2[k][1]])
                    pu = bank_aux.next()
                    proj(pu, 512, W, slice(256 + c2 * 128, 256 + (c2 + 1) * 128), lambda k: u2[k][0], 8, wb, lambda k: [u2[k][1]])
                    t, tb = tmpf.next()
                    ACT(t[:, 0:512], ps[pg][:, 0:512], AF.Silu, [bps[pg]], [tb])
                    TT(hm[hc][0], t[:, 0:512], ps[pu][:, 0:512], ALU.mult, [tb, bps[pu]], [hm[hc][1]])
            for j in range(8):
                W, wb = ws.get((l, "dn", j))
                pb = bank_main.next()
                proj(pb, 512, W, slice(0, 128), lambda k: hm[k][0], NFF, wb, lambda k: [hm[k][1]])
                xo = xT[:, j, own0:own0 + 512]
                STT(xo, ps[pb][:, 0:512], mod(5, j), xo, ALU.mult, ALU.add, [bps[pb], bmod, xb_own[j]], [xb_own[j]])

            checkpoint("ffn")
            if is_last:
                arena.reset()
                yT = [arena.alloc("yT%d" % j, 512, F32) for j in range(8)]
                stg = [arena.alloc("stg%d" % i, 1024, F32) for i in range(2)]
                rms_modulate(512, lambda j: xT[:, j, own0:own0 + 512], lambda j: xb_own[j],
                             lambda j: yT[j][0], lambda j: yT[j][1], lambda j: fing[:, j:j + 1], lambda j: None)
                ydst = ys_d if kind == "S" else yp_d
                for tc in range(4):
                    sg_, sgb_ = stg[tc % 2]
                    for half in range(2):
                        pb = bank_main.next()
                        for q in range(4):
                            j = 4 * half + q
                            TR(ps[pb][:, q * 128:(q + 1) * 128], yT[j][0][:, tc * 128:(tc + 1) * 128], identf[:],
                               [yT[j][1], bconst], [bps[pb]])
                        CPY("act" if half == 0 else "dve", sg_[:, half * 512:(half + 1) * 512], ps[pb][:, 0:512], [bps[pb]], [sgb_])
                    DMA("pool", ydst[own0 + tc * 128:own0 + tc * 128 + 128, :], sg_, [sgb_], [], final=True)

        def load_x(src_d, ntc):
            arena.reset()
            stg = [arena.alloc("xin%d" % i, 1024, F32) for i in range(2)]
            for tc in range(ntc):
                sg_, sgb_ = stg[tc % 2]
                DMA("sp", sg_, src_d[tc * 128:(tc + 1) * 128, :], [], [sgb_])
                t = tc // 4
                for half in range(2):
                    pb = bank_main.next()
                    for q in range(4):
                        j = 4 * half + q
                        TR(ps[pb][:, q * 128:(q + 1) * 128], sg_[:, j * 128:(j + 1) * 128], identf[:], [sgb_, bconst], [bps[pb]])
                    CPY("act" if half == 0 else "dve", xT[:, 4 * half:4 * half + 4, tc * 128:(tc + 1) * 128],
                        ps[pb][:, 0:512].rearrange("p (a b) -> p a b", b=128), [bps[pb]], [bx[t][4 * half + q] for q in range(4)])

        per_tile_casts = (12 + 61 + s_tiles - 1) // s_tiles

        def emit_all():
            emit_consts()
            checkpoint("consts")
            if do_S:
                load_x(xs_d, min(16, 4 * (s_tiles + 1)))
                checkpoint("load")
                tables_pre()
                ws.cast_some(12 + 61)
                adaln(0)
                checkpoint("ada")
                build_layer_tables(0)
                checkpoint("tables")
                norm1(0, "S", 0)
                for l in range(depth):
                    for ti in range(s_tiles):
                        def hoist(l=l, ti=ti):
                            if ti + 1 < s_tiles:
                                norm1(l, "S", ti + 1)
                            elif l + 1 < depth:
                                adaln(l + 1)
                                build_layer_tables(l + 1)
                                norm1(l + 1, "S", 0)
                            elif do_P:
                                load_x(xp_d, 4)
                                norm1(0, "P", 0)
                        ws.cast_some(per_tile_casts)
                        tile_prog(l, "S", ti, l == depth - 1, hoist)
            if do_P:
                if not do_S:
                    ws.cast_some(12 + 61)
                    load_x(xp_d, 4)
                    checkpoint("load")
                    for l in range(depth):
                        adaln(l)
                    checkpoint("ada")
                    norm1(0, "P", 0)
                for l in range(depth):
                    if l > 0:
                        norm1(l, "P", 0)
                    tile_prog(l, "P", 0, l == depth - 1, lambda: None)

        S.plan = True
        try:
            emit_all()
        except StopBuild:
            pass
        seq = ws.seq
        S.plan = False
        S.arena_bufs = []
        ws = WS()
        ws.seq = seq
        seen = set()
        for bid in seq:
            if bid not in seen:
                seen.add(bid)
                ws.cast_queue.append(bid)
        for r_ in (tmpf, tmpb, rsd, bank_main, bank_aux, bank_s):
            r_.i = 0
        DUMPS.clear()
        try:
            emit_all()
            assert ws.pos == len(ws.seq), (ws.pos, len(ws.seq))
        except StopBuild as ex:
            print("STOPPED at", ex)
        for (name, ap, bufs) in DUMPS:
            dd = nc.dram_tensor("dbg_" + name, list(ap.shape), ap.dtype, kind="ExternalOutput").ap()
            DMA("sp", dd, ap, bufs, [], final=True)
        S.emit()
    return nc, S


def _fm(v):
    v = np.asarray(v, np.float32)
    lead = v.shape[:-1]
    r = v.reshape(lead + (8, 128))
    r = np.moveaxis(r, -1, 0)
    return np.ascontiguousarray(r)


_PROG = {}


def _host_consts():
    cols = np.arange(GW)
    cs = np.clip(cols - 8, 0, GW - 16)
    cm = np.full((128, 64), NEG, np.float32)
    for w in range(GW):
        for s in range(2):
            cm[64 * s + cs[w]:64 * s + cs[w] + 16, w] = 0.0
    jrev = np.zeros((128, 128), np.float32)
    for s in range(2):
        for c in range(64):
            jrev[64 * s + c, 64 * s + 63 - c] = 1.0
    return cm, np.eye(128, dtype=np.float32), jrev


def make_in_maps(inputs, depth=DEPTH):
    f32 = lambda a: np.ascontiguousarray(np.asarray(a, np.float32))
    x_prompt, x_sample = f32(inputs["x_prompt"]), f32(inputs["x_sample"])
    cache_k, cache_v = f32(inputs["cache_k"]), f32(inputs["cache_v"])
    c, c_ctx = f32(inputs["c"]), f32(inputs["c_ctx"])
    cm, ident, jrev = _host_consts()
    rpb = f32(inputs["na_rpb"])
    rpbR = np.zeros((DEPTH, NH, 128, 15), np.float32)
    rpbR[:, :, 48:79, :] = rpb[:, :, ::-1, ::-1].transpose(0, 1, 3, 2)
    bada = np.ascontiguousarray(np.asarray(inputs["b_ada"], np.float32).reshape(DEPTH, 48, 128).transpose(2, 0, 1))
    vec8 = np.stack([_fm(inputs[n]) for n in ("norm1_g", "norm2_g", "conv_dw_b", "conv_ln_g", "conv_ln_b")], axis=2)
    fing = _fm(inputs["final_g"])
    dww = np.ascontiguousarray(_fm(inputs["conv_dw_w"]).transpose(0, 1, 3, 2))
    scw = np.ascontiguousarray(_fm(inputs["sc_dw_w"]).transpose(0, 1, 3, 2))
    shared = dict(bada=bada, vec8=np.ascontiguousarray(vec8), fing=fing, dww=dww, scw=scw, rpbR=rpbR, cmask=cm, identf=ident,
                  jrev=jrev, w_ada=f32(inputs["w_ada"]), w_in=f32(inputs["w_in"]), conv_pw_w=f32(inputs["conv_pw_w"]),
                  sc_out_w=f32(inputs["sc_out_w"]), na_out_w=f32(inputs["na_out_w"]), w_o=f32(inputs["w_o"]),
                  ffn_w_gate=f32(inputs["ffn_w_gate"]), ffn_w_up=f32(inputs["ffn_w_up"]), ffn_w_down=f32(inputs["ffn_w_down"]))
    maps = []
    for core in range(8):
        cond = np.stack([c[core], c_ctx], axis=0)
        condT = np.ascontiguousarray(cond.reshape(2, 8, 128).transpose(2, 1, 0))
        m = dict(shared)
        m.update(xs=x_sample[core], xp=np.ascontiguousarray(x_prompt[2 * core:2 * core + 2].reshape(512, D)),
                 ck=cache_k[core], cv=cache_v[core], condT=condT)
        maps.append(m)
    return maps


def kernel(**inputs):
    if "nc" not in _PROG:
        _PROG["nc"], _ = build_program()
    nc = _PROG["nc"]
    maps = make_in_maps(inputs)
    res = run_bass_kernel_spmd(nc, maps, core_ids=list(range(8)))
    r = res.results
    y_sample = np.stack([r[i]["ys"] for i in range(8)], axis=0)
    y_prompt = np.concatenate([r[i]["yp"].reshape(2, 256, D) for i in range(8)], axis=0)
    new_k = np.concatenate([r[i]["nk"] for i in range(8)], axis=0)
    new_v = np.concatenate([r[i]["nv"] for i in range(8)], axis=0)
    return (y_prompt.astype(np.float32), y_sample.astype(np.float32), new_k.astype(np.float32), new_v.astype(np.float32))
```
